# Optimizing a Trainium2 kernel written in Bass

```python
import math
import jax, jax.numpy as jnp
from jax import lax
import numpy as np

D_MODEL = 1024
BATCH = 16
SEQ = 256
DEPTH = 2
DEC_BATCH = 2
DEC_SEQ = 2048
PAST_LEN = 256

GRID_W = 64
Q_BLOCK = 128
ROPE_THETA = 10000.0
NORM_EPS = 1e-6

DIFF_HEADS = 4
DIFF_HEAD_DIM = 64
DIFF_V_DIM = 2 * DIFF_HEAD_DIM
DIFF_QK_WIDTH = DIFF_HEADS * 2 * DIFF_HEAD_DIM
DIFF_WIDTH = DIFF_HEADS * DIFF_V_DIM

GQA_HEADS = 8
GQA_KV_HEADS = 2
GQA_GROUP = GQA_HEADS // GQA_KV_HEADS
GQA_HEAD_DIM = 64
GQA_WIDTH = GQA_HEADS * GQA_HEAD_DIM
GQA_KV_WIDTH = GQA_KV_HEADS * GQA_HEAD_DIM

CONV_WIDTH = 512
CONV_KSIZE = 31

N_BRANCH = 3
MLP_HIDDEN = 4 * D_MODEL
N_MOD = 6

IN_SIZES = (DIFF_QK_WIDTH, DIFF_QK_WIDTH, DIFF_WIDTH,
            GQA_WIDTH, GQA_KV_WIDTH, GQA_KV_WIDTH,
            2 * CONV_WIDTH, N_BRANCH * D_MODEL)
IN_WIDTH = sum(IN_SIZES)

kernel_name = "hybrid_diff_gqa_conformer_dit_step"


def rms_norm(x, g):
    xf = x.astype(jnp.float32)
    y = xf * lax.rsqrt(jnp.mean(xf * xf, axis=-1, keepdims=True) + NORM_EPS)
    return (y * g.astype(jnp.float32)).astype(x.dtype)


def layer_norm(x, g, b):
    xf = x.astype(jnp.float32)
    mu = jnp.mean(xf, axis=-1, keepdims=True)
    xc = xf - mu
    y = xc * lax.rsqrt(jnp.mean(xc * xc, axis=-1, keepdims=True) + NORM_EPS)
    return (y * g.astype(jnp.float32) + b.astype(jnp.float32)).astype(x.dtype)


def axial_rope_tables(n_tokens, dim):
    n_rows = n_tokens // GRID_W
    row = jnp.repeat(jnp.arange(n_rows), GRID_W).astype(jnp.float32)
    col = jnp.tile(jnp.arange(GRID_W), n_rows).astype(jnp.float32)
    axis_dim = dim // 2
    freqs = ROPE_THETA ** (-jnp.arange(0, axis_dim, 2, dtype=jnp.float32) / axis_dim)
    ang_r = row[:, None] * freqs[None, :]
    ang_c = col[:, None] * freqs[None, :]
    ang = jnp.concatenate([ang_r, ang_r, ang_c, ang_c], axis=-1)
    return jnp.cos(ang), jnp.sin(ang)


def apply_axial_rope(x, rope):
    cos, sin = rope
    x1, x2, x3, x4 = jnp.split(x, 4, axis=-1)
    rot = jnp.concatenate([-x2, x1, -x4, x3], axis=-1)
    shape = (1, x.shape[1]) + (1,) * (x.ndim - 3) + (x.shape[-1],)
    return x * cos.reshape(shape).astype(x.dtype) + rot * sin.reshape(shape).astype(x.dtype)


def sweep_query_blocks(fn, q):
    b, s = q.shape[0], q.shape[1]
    nb = s // Q_BLOCK
    qb = jnp.moveaxis(q.reshape((b, nb, Q_BLOCK) + q.shape[2:]), 1, 0)
    out = lax.map(fn, qb)
    out = jnp.moveaxis(out, 0, 1)
    return out.reshape((b, s) + out.shape[3:])


def differential_attention(q, k, v, lam):
    scale = DIFF_HEAD_DIM ** -0.5
    kf = k.astype(jnp.float32)
    vf = v.astype(jnp.float32)

    def block(qb):
        s = jnp.einsum('bqhcd,bkhcd->bhcqk', qb.astype(jnp.float32), kf) * scale
        p = jax.nn.softmax(s, axis=-1)
        a = p[:, :, 0] - lam * p[:, :, 1]
        return jnp.einsum('bhqk,bkhe->bqhe', a, vf).astype(q.dtype)

    return sweep_query_blocks(block, q)


def grouped_query_attention(q, k, v):
    scale = GQA_HEAD_DIM ** -0.5
    kf = k.astype(jnp.float32)
    vf = v.astype(jnp.float32)

    def block(qb):
        s = jnp.einsum('bqngd,bknd->bngqk', qb.astype(jnp.float32), kf) * scale
        p = jax.nn.softmax(s, axis=-1)
        return jnp.einsum('bngqk,bknd->bqngd', p, vf).astype(q.dtype)

    return sweep_query_blocks(block, q)


def split_points():
    pts, acc = [], 0
    for s in IN_SIZES[:-1]:
        acc += s
        pts.append(acc)
    return pts


def parallel_mixer(h, l, P, ctx, rope_d, rope_g):
    b, s, _ = h.shape
    proj = h @ P['w_in'][l]
    dq, dk, dv, gq, gk, gv, cv, gl = jnp.split(proj, split_points(), axis=-1)

    dq = dq.reshape(b, s, DIFF_HEADS, 2, DIFF_HEAD_DIM)
    dk = dk.reshape(b, s, DIFF_HEADS, 2, DIFF_HEAD_DIM)
    dv = dv.reshape(b, s, DIFF_HEADS, DIFF_V_DIM)
    gq = rms_norm(gq.reshape(b, s, GQA_KV_HEADS, GQA_GROUP, GQA_HEAD_DIM), P['gqa_q_norm'][l])
    gk = rms_norm(gk.reshape(b, s, GQA_KV_HEADS, GQA_HEAD_DIM), P['gqa_k_norm'][l])
    gv = gv.reshape(b, s, GQA_KV_HEADS, GQA_HEAD_DIM)
    own_ctx = (dk, dv, gk, gv)

    if ctx is None:
        dk_all, dv_all, gk_all, gv_all = dk, dv, gk, gv
    else:
        dq = apply_axial_rope(dq, rope_d)
        gq = apply_axial_rope(gq, rope_g)
        dk_all = jnp.concatenate([apply_axial_rope(dk, rope_d), ctx[0].astype(dk.dtype)], axis=1)
        dv_all = jnp.concatenate([dv, ctx[1].astype(dv.dtype)], axis=1)
        gk_all = jnp.concatenate([apply_axial_rope(gk, rope_g), ctx[2].astype(gk.dtype)], axis=1)
        gv_all = jnp.concatenate([gv, ctx[3].astype(gv.dtype)], axis=1)

    lam_init = 0.8 - 0.6 * math.exp(-0.3 * l)
    f32 = jnp.float32
    lam = (jnp.exp(jnp.sum(P['diff_lq1'][l].astype(f32) * P['diff_lk1'][l].astype(f32)))
           - jnp.exp(jnp.sum(P['diff_lq2'][l].astype(f32) * P['diff_lk2'][l].astype(f32)))
           + lam_init)
    da = differential_attention(dq, dk_all, dv_all, lam)
    da = rms_norm(da, P['diff_subln'][l]) * (1.0 - lam_init)
    branch_a = da.reshape(b, s, DIFF_WIDTH) @ P['w_diff_o'][l]

    ga = grouped_query_attention(gq, gk_all, gv_all)
    branch_b = ga.reshape(b, s, GQA_WIDTH) @ P['w_gqa_o'][l]

    u = cv[..., :CONV_WIDTH] * jax.nn.sigmoid(cv[..., CONV_WIDTH:])
    kern = P['conv_dw'][l][:, None, :].astype(u.dtype)
    u = lax.conv_general_dilated(u, kern, window_strides=(1,),
                                 padding=[(CONV_KSIZE // 2, CONV_KSIZE // 2)],
                                 dimension_numbers=('NWC', 'WIO', 'NWC'),
                                 feature_group_count=CONV_WIDTH)
    u = u + P['conv_dw_b'][l]
    u = jax.nn.silu(layer_norm(u, P['conv_ln_g'][l], P['conv_ln_b'][l]))
    branch_c = u @ P['w_conv_o'][l]

    g = jax.nn.sigmoid(gl.reshape(b, s, N_BRANCH, D_MODEL))
    merged = g[:, :, 0] * branch_a + g[:, :, 1] * branch_b + g[:, :, 2] * branch_c
    return merged @ P['w_o'][l], own_ctx


def trunk_layer(x, mod, l, P, ctx, rope_d, rope_g):
    shift1, scale1, gate1, shift2, scale2, gate2 = jnp.split(mod, N_MOD, axis=-1)
    h = rms_norm(x, P['norm1'][l]) * (1.0 + scale1) + shift1
    m, own_ctx = parallel_mixer(h, l, P, ctx, rope_d, rope_g)
    x = x + gate1 * m
    h = rms_norm(x, P['norm2'][l]) * (1.0 + scale2) + shift2
    f = jnp.square(jax.nn.relu(h @ P['w_mlp1'][l])) @ P['w_mlp2'][l]
    x = x + gate2 * f
    return x, own_ctx


def setup_inputs(seed: int = 0) -> dict:
    key = jax.random.key(seed)
    ks = jax.random.split(key, 32)
    f32 = jnp.float32

    def nrm(k, shape, scale):
        return jax.random.normal(k, shape, f32) * scale

    return {
        'x_prompt': nrm(ks[0], (BATCH, SEQ, D_MODEL), 1.0),
        'x_sample': nrm(ks[1], (DEC_BATCH, DEC_SEQ, D_MODEL), 1.0),
        'cache_diff_k': nrm(ks[2], (DEC_BATCH, DEPTH, PAST_LEN, DIFF_HEADS, 2, DIFF_HEAD_DIM), 1.0),
        'cache_diff_v': nrm(ks[3], (DEC_BATCH, DEPTH, PAST_LEN, DIFF_HEADS, DIFF_V_DIM), 1.0),
        'cache_gqa_k': nrm(ks[4], (DEC_BATCH, DEPTH, PAST_LEN, GQA_KV_HEADS, GQA_HEAD_DIM), 1.0),
        'cache_gqa_v': nrm(ks[5], (DEC_BATCH, DEPTH, PAST_LEN, GQA_KV_HEADS, GQA_HEAD_DIM), 1.0),
        'c': nrm(ks[6], (DEC_BATCH, D_MODEL), 1.0),
        'c_ctx': nrm(ks[7], (D_MODEL,), 1.0),
        'w_ada': nrm(ks[8], (DEPTH, D_MODEL, N_MOD * D_MODEL), 0.5 * D_MODEL ** -0.5),
        'b_ada': nrm(ks[9], (DEPTH, N_MOD * D_MODEL), 0.02),
        'norm1': 1.0 + nrm(ks[10], (DEPTH, D_MODEL), 0.02),
        'norm2': 1.0 + nrm(ks[11], (DEPTH, D_MODEL), 0.02),
        'w_in': nrm(ks[12], (DEPTH, D_MODEL, IN_WIDTH), D_MODEL ** -0.5),
        'diff_lq1': nrm(ks[13], (DEPTH, DIFF_HEAD_DIM), 0.1),
        'diff_lk1': nrm(ks[14], (DEPTH, DIFF_HEAD_DIM), 0.1),
        'diff_lq2': nrm(ks[15], (DEPTH, DIFF_HEAD_DIM), 0.1),
        'diff_lk2': nrm(ks[16], (DEPTH, DIFF_HEAD_DIM), 0.1),
        'diff_subln': 1.0 + nrm(ks[17], (DEPTH, DIFF_V_DIM), 0.02),
        'w_diff_o': nrm(ks[18], (DEPTH, DIFF_WIDTH, D_MODEL), DIFF_WIDTH ** -0.5),
        'gqa_q_norm': 1.0 + nrm(ks[19], (DEPTH, GQA_HEAD_DIM), 0.02),
        'gqa_k_norm': 1.0 + nrm(ks[20], (DEPTH, GQA_HEAD_DIM), 0.02),
        'w_gqa_o': nrm(ks[21], (DEPTH, GQA_WIDTH, D_MODEL), GQA_WIDTH ** -0.5),
        'conv_dw': nrm(ks[22], (DEPTH, CONV_KSIZE, CONV_WIDTH), CONV_KSIZE ** -0.5),
        'conv_dw_b': nrm(ks[23], (DEPTH, CONV_WIDTH), 0.02),
        'conv_ln_g': 1.0 + nrm(ks[24], (DEPTH, CONV_WIDTH), 0.02),
        'conv_ln_b': nrm(ks[25], (DEPTH, CONV_WIDTH), 0.02),
        'w_conv_o': nrm(ks[26], (DEPTH, CONV_WIDTH, D_MODEL), CONV_WIDTH ** -0.5),
        'w_o': nrm(ks[27], (DEPTH, D_MODEL, D_MODEL), D_MODEL ** -0.5),
        'w_mlp1': nrm(ks[28], (DEPTH, D_MODEL, MLP_HIDDEN), D_MODEL ** -0.5),
        'w_mlp2': nrm(ks[29], (DEPTH, MLP_HIDDEN, D_MODEL), MLP_HIDDEN ** -0.5),
        'final_norm': 1.0 + nrm(ks[30], (D_MODEL,), 0.02),
    }


def reference(x_prompt, x_sample, cache_diff_k, cache_diff_v, cache_gqa_k, cache_gqa_v, c, c_ctx,
              w_ada, b_ada, norm1, norm2, w_in, diff_lq1, diff_lk1, diff_lq2, diff_lk2, diff_subln,
              w_diff_o, gqa_q_norm, gqa_k_norm, w_gqa_o, conv_dw, conv_dw_b, conv_ln_g, conv_ln_b,
              w_conv_o, w_o, w_mlp1, w_mlp2, final_norm):
    P = dict(norm1=norm1, norm2=norm2, w_in=w_in, diff_lq1=diff_lq1, diff_lk1=diff_lk1,
             diff_lq2=diff_lq2, diff_lk2=diff_lk2, diff_subln=diff_subln, w_diff_o=w_diff_o,
             gqa_q_norm=gqa_q_norm, gqa_k_norm=gqa_k_norm, w_gqa_o=w_gqa_o, conv_dw=conv_dw,
             conv_dw_b=conv_dw_b, conv_ln_g=conv_ln_g, conv_ln_b=conv_ln_b, w_conv_o=w_conv_o,
             w_o=w_o, w_mlp1=w_mlp1, w_mlp2=w_mlp2)

    x = x_prompt
    ctx_layers = []
    for l in range(DEPTH):
        mod = (jax.nn.silu(c_ctx) @ w_ada[l] + b_ada[l])[None, None, :]
        x, own_ctx = trunk_layer(x, mod, l, P, None, None, None)
        ctx_layers.append(own_ctx)
    y_prompt = rms_norm(x, final_norm)
    new_diff_k = jnp.stack([t[0] for t in ctx_layers], axis=1)
    new_diff_v = jnp.stack([t[1] for t in ctx_layers], axis=1)
    new_gqa_k = jnp.stack([t[2] for t in ctx_layers], axis=1)
    new_gqa_v = jnp.stack([t[3] for t in ctx_layers], axis=1)

    n_lat = x_sample.shape[1]
    rope_d = axial_rope_tables(n_lat, DIFF_HEAD_DIM)
    rope_g = axial_rope_tables(n_lat, GQA_HEAD_DIM)
    x = x_sample
    for l in range(DEPTH):
        mod = (jax.nn.silu(c) @ w_ada[l] + b_ada[l])[:, None, :]
        ctx = (cache_diff_k[:, l], cache_diff_v[:, l], cache_gqa_k[:, l], cache_gqa_v[:, l])
        x, _ = trunk_layer(x, mod, l, P, ctx, rope_d, rope_g)
    y_sample = rms_norm(x, final_norm)

    return (y_prompt, y_sample, new_diff_k, new_diff_v, new_gqa_k, new_gqa_v)
```

```python
import math
import numpy as np
import concourse.bass as bass
import concourse.mybir as mybir
from concourse.bass_utils import run_bass_kernel_spmd

F32 = mybir.dt.float32
BF16 = mybir.dt.bfloat16
ALU = mybir.AluOpType
AF = mybir.ActivationFunctionType
AX = mybir.AxisListType

ENGS = ['pe', 'act', 'dve', 'pool', 'sp']
D = 1024
L = 2
EPS = 1e-6
NU = 13
FROW = 12 * 512 + 128
PL = 459
NPP = 2 * PL + 8
NOCOLL = False
DBG = {}
NOROPE = False
ROPEL = 9
STOP = None


class StopBuild(Exception):
    pass


def chk(name):
    if STOP == name:
        raise StopBuild()


class Buf:
    __slots__ = ('name', 'w', 'r', 'excl')

    def __init__(self, name='', excl=False):
        self.name = name
        self.w = []
        self.r = []
        self.excl = excl


class Op:
    __slots__ = ('eng', 'fn', 'deps', 'dma', 'sig', 'sem', 'val', 'guard')

    def __init__(self, eng, fn, dma):
        self.eng = eng
        self.fn = fn
        self.dma = dma
        self.deps = []
        self.sig = False
        self.sem = None
        self.val = None
        self.guard = None


class Sched:
    NDMA = {'sp': 24, 'pool': 2, 'act': 2}

    def __init__(self, nc):
        self.nc = nc
        self.ops = {e: [] for e in ENGS}
        self.alldma = []

    def op(self, eng, fn, reads=(), writes=(), dma=False):
        o = Op(eng, fn, dma)
        deps = {}

        def add(d, raw):
            if d is o:
                return
            if eng == 'pe' and d.eng == 'pe' and (not dma) and (not d.dma):
                return
            deps[id(d)] = d

        for b in reads:
            for d in b.w:
                add(d, True)
            if b.excl:
                for d in b.r:
                    if d.eng != eng:
                        add(d, True)
        for b in writes:
            for d in b.w:
                add(d, False)
            for d in b.r:
                add(d, False)
        o.deps = list(deps.values())
        for d in o.deps:
            d.sig = True
        for b in writes:
            b.w = [o]
            b.r = []
        for b in reads:
            if b in writes:
                continue
            if not dma:
                b.r = [x for x in b.r if x.dma or x.eng != eng]
            b.r.append(o)
        self.ops[eng].append(o)
        if dma:
            self.alldma.append(o)
        return o

    @staticmethod
    def alias(dst, src):
        w = {}
        r = {}
        for s_ in src:
            for d in s_.w:
                w[id(d)] = d
            for d in s_.r:
                r[id(d)] = d
        for d in dst:
            d.w = list(w.values())
            d.r = list(r.values())

    def emit(self):
        nc = self.nc
        esem = {e: nc.alloc_semaphore('s_' + e) for e in ENGS}
        dsem = {e: [nc.alloc_semaphore('d_%s%d' % (e, i)) for i in range(n)] for e, n in self.NDMA.items()}
        dcount = {e: [0] * n for e, n in self.NDMA.items()}
        kk = {e: 0 for e in self.NDMA}
        for o in self.alldma:
            e = o.eng
            j = kk[e] % self.NDMA[e]
            kk[e] += 1
            inc = 16 if o.dma is True else o.dma
            dcount[e][j] += inc
            o.sem = dsem[e][j]
            o.val = dcount[e][j]
            o.guard = (o.sem, o.val - inc) if o.val > inc else None
        for e in ENGS:
            c = 0
            for o in self.ops[e]:
                if o.dma:
                    continue
                if o.sig:
                    c += 1
                    o.sem = esem[e]
                    o.val = c
        final = []
        for e in self.NDMA:
            for j in range(self.NDMA[e]):
                if dcount[e][j]:
                    final.append((dsem[e][j], dcount[e][j]))

        def run(e):
            def body(eng):
                seen = {}
                for o in self.ops[e]:
                    waits = {}
                    for d in o.deps:
                        key = d.sem.num
                        if waits.get(key, (None, 0))[1] < d.val:
                            waits[key] = (d.sem, d.val)
                    if o.guard is not None:
                        key = o.guard[0].num
                        if waits.get(key, (None, 0))[1] < o.guard[1]:
                            waits[key] = o.guard
                    for key, (sem, val) in waits.items():
                        if seen.get(key, 0) >= val:
                            continue
                        seen[key] = val
                        eng.wait_ge(sem, val)
                    ins = o.fn(eng)
                    if o.dma:
                        ins.then_inc(o.sem, 16 if o.dma is True else o.dma)
                    elif o.sig:
                        ins.then_inc(o.sem, 1)
                if e == 'sp':
                    for sem, v in final:
                        if seen.get(sem.num, 0) < v:
                            eng.wait_ge(sem, v)
            return body

        with nc.Block() as block:
            block.tensor(run('pe'))
            block.scalar(run('act'))
            block.vector(run('dve'))
            block.gpsimd(run('pool'))
            block.sync(run('sp'))


def build():
    nc = bass.Bass("TRN2", target_bir_lowering=False)
    s = Sched(nc)

    def din(name, shape, dt=F32):
        return nc.dram_tensor(name, shape, dt, kind="ExternalInput").ap()

    def dout(name, shape):
        return nc.dram_tensor(name, shape, F32, kind="ExternalOutput").ap()

    xT_d = din("xT", [D, 1024])
    cT_d = din("cT", [128, 16])
    pp_d = din("pp", [128, NPP])
    cs_d = din("cs", [128, 1024])
    mats_d = din("mats", [128, 384])
    masks_d = din("masks", [128, 18])
    ckd_d = din("ckd", [L, 512, 256])
    ckg_d = din("ckg", [L, 2, 128, 256])
    cvd_d = din("cvd", [L, 256, 512])
    cvg_d = din("cvg", [L, 256, 256])
    w_ada = din("w_ada", [L, D, 6144])
    w_in = din("w_in", [L, D, 6400])
    w_diff_o = din("w_diff_o", [L, 512, D])
    w_gqa_o = din("w_gqa_o", [L, 512, D])
    w_conv_o = din("w_conv_o", [L, 512, D])
    w_o = din("w_o", [L, D, D])
    w_mlp1 = din("w_mlp1", [L, D, 4096])
    w_mlp2 = din("w_mlp2", [L, 4096, D])
    yT_d = dout("yT", [D, 1024])
    okd_d = dout("okd", [L, 512, 512])
    okg_d = dout("okg", [L, 128, 512])
    ovd_d = dout("ovd", [L, 512, 512])
    ovg_d = dout("ovg", [L, 512, 128])
    xsT_d = din("xsT", [4, D, 512])
    cs4_d = din("cs4", [4, 128, 1024])
    kv_d = [nc.dram_tensor("kv%d" % l, [4 * 128, FROW], BF16).ap() for l in range(L)]
    xs1_d = nc.dram_tensor("xs1", [4, D, 512], F32).ap()
    NWD = 112
    wcd_d = nc.dram_tensor("wcd", [NWD, 128, 2048], BF16).ap()

    cnt = [0]

    def sb(shape, dt, name=None):
        cnt[0] += 1
        return nc.alloc_sbuf_tensor("sb_" + (name or ("t%d" % cnt[0])), shape, dt)

    x = sb([128, 8, 1024], F32, "x")
    xb = [[Buf() for _ in range(2)] for _ in range(8)]
    h = sb([128, 8, 1024], BF16, "h")
    hb = [[Buf() for _ in range(2)] for _ in range(8)]
    pp = sb([128, NPP], F32, "pp")
    ppb = Buf()
    cs = sb([128, 1024], F32, "cs")
    csb = Buf()
    matsf = sb([128, 384], F32, "matsf")
    matsb = sb([128, 384], BF16, "matsb")
    matb = Buf()
    masks = sb([128, 18], F32, "masks")
    maskb = Buf()
    cT = sb([128, 16], F32, "cT")
    scT = sb([128, 16], BF16, "scT")
    ctb = Buf()
    mod = sb([128, L, 48, 2], F32, "mod")
    modb = Buf()
    drv = sb([128, L, 2, 6, 8], F32, "drv")
    drvb = Buf()
    lam = sb([128, L, 4], F32, "lam")
    lamb = Buf()
    epsb = sb([128, 1], F32, "epsb")
    epsbb = Buf()
    ones_f = matsf[:, 256:384]
    rot_b = matsb[:, 0:128]
    bd_f = matsf[:, 128:256]
    ones_b = matsb[:, 256:384]

    NST, NWB = 2, 3
    wst = [sb([128, 2048], F32, "wst%d" % i) for i in range(NST)]
    wstb = [Buf() for _ in range(NST)]
    wbf = [sb([128, 2048], BF16, "wbf%d" % i) for i in range(NWB)]
    wbfb = [Buf() for _ in range(NWB)]
    wk = [0, 0]

    NT = 6
    tf = [sb([128, 512], F32, "tf%d" % i) for i in range(NT)]
    tfb = [Buf() for _ in range(NT)]
    LT = [sb([128, 512], F32, "lt%d" % i) for i in range(3)]
    LTb = [Buf() for _ in range(3)]
    tk = [0]
    NE = 6
    te = [sb([128, 512], BF16, "te%d" % i) for i in range(NE)]
    teb = [Buf() for _ in range(NE)]
    ek = [0]

    def tmpf():
        i = tk[0] % NT
        tk[0] += 1
        return tf[i], tfb[i]

    def tmpe():
        i = ek[0] % NE
        ek[0] += 1
        return te[i], teb[i]

    pst = [nc.alloc_psum_tensor("ps%d" % i, [128, 512], F32) for i in range(8)]
    psb = [Buf(excl=True) for _ in range(8)]
    pk = [0]

    def psum():
        i = pk[0] % 4
        pk[0] += 1
        return pst[i], psb[i]

    ak = [0]

    def psum_acc():
        i = 4 + (ak[0] % 4)
        ak[0] += 1
        return pst[i], psb[i]

    RB = 92 * 1024
    reg = sb([128, RB // 2], BF16, "region")

    def rview(off, nbytes, dt, pattern=None, **kw):
        v = reg[:, off // 2:(off + nbytes) // 2]
        if dt == F32:
            v = v.bitcast(F32)
        if pattern:
            v = v.rearrange(pattern, **kw)
        return v

    o = 0
    dqT = rview(o, 8192, BF16, "p (c n) -> p c n", c=4); o += 8192
    gqT = rview(o, 8192, BF16, "p (c n) -> p c n", c=4); o += 8192
    dkT = rview(o, 8192, BF16, "p (c n) -> p c n", c=4); o += 8192
    gkT = rview(o, 4096, BF16, "p (c n) -> p c n", c=2); o += 4096
    Vd = rview(o, 8192, BF16, "p (t n) -> p t n", t=8); o += 8192
    Vg = rview(o, 4096, BF16, "p (t n) -> p t n", t=8); o += 4096
    UW = 1114
    upad = rview(o, 4 * UW * 4, F32, "p (c n) -> p c n", c=4); o += 4 * UW * 4
    o = (o + 63) // 64 * 64
    edge = rview(o, 512, BF16); o += 512
    xtmp_off = o
    xtmp = rview(o, 8192, BF16, "p (j n) -> p j n", j=8); o += 13312
    gK = rview(xtmp_off, 2304 * 2, BF16)
    gV = rview(xtmp_off + 4608, 18 * 128 * 2, BF16, "p (t n) -> p t n", t=18)
    gA = rview(xtmp_off + 9216, 4096, BF16, "p (a n) -> p a n", a=4)
    ucT = rview(xtmp_off, 8192, BF16, "p (c n) -> p c n", c=4)
    off_da = o
    daT = rview(o, 8192, BF16, "p (c n) -> p c n", c=4); o += 8192
    gaT = rview(o, 8192, BF16, "p (c n) -> p c n", c=4); o += 8192
    sgv = rview(off_da, 16384, F32, "p (c n) -> p c n", c=4)
    cacc = rview(o, 4352, F32); o += 4352
    assert o <= RB, o
    DBG.update(dict(off_da=off_da, xtmp_off=xtmp_off, upad_off=40960, UW=UW))
    merged = rview(0, 16384, BF16, "p (c n) -> p c n", c=8)
    hid = rview(16384, 65536, BF16, "p (c n) -> p c n", c=32)
    hedge = sb([128, 8, 30], BF16, "hedge")
    hedgeb = Buf()
    BL = [0, 1]
    KVB = [0]
    WC = {'on': False, 'map': {}, 'n': 0}
    WD = {'mode': 'off', 'map': {}}
    wcd_buf = [Buf() for _ in range(112)]
    wc_offs = [40960, 45056, 49152, 53248, 59328, 63424, 67520, 72640, 76736, 80832, 84928]
    wc_view = [rview(o_, 4096, BF16) for o_ in wc_offs]
    wc_buf = [Buf() for _ in wc_offs]

    B = {k: Buf(k) for k in ['dqT0', 'dqT1', 'gqT0', 'gqT1', 'dkT0', 'dkT1', 'gkT0', 'gkT1', 'Vd0', 'Vd1', 'Vg0', 'Vg1',
                             'upad', 'edge', 'xtmp', 'gK', 'gV', 'gA', 'daT', 'gaT', 'ucT', 'cacc', 'halo',
                             'merged', 'hid0', 'hid1', 'xs1_0', 'xs1_1', 'xs1_2', 'xs1_3', 'out']}
    for l_ in range(L):
        for i_ in range(4):
            B['kv%d_%d' % (l_, i_)] = Buf()

    def V(op, *a, **k):
        return op

    def dma(out, in_, reads=(), writes=()):
        s.op('sp', lambda e: e.dma_start(out=out, in_=in_), reads=reads, writes=writes, dma=True)

    def load_w(src):
        rows, n = src.shape
        kc = rows // 128
        assert kc * n <= 2048, (kc, n)
        key = repr(src)
        if WC['on'] and key in WC['map']:
            return WC['map'][key]
        if WD['mode'] == 'read' and key in WD['map']:
            di = WD['map'][key]
            j = wk[1] % NWB
            wk[1] += 1
            dma(wbf[j][:, 0:kc * n], wcd_d[di][:, 0:kc * n], reads=[wcd_buf[di]], writes=[wbfb[j]])
            return wbf[j][:, 0:kc * n].rearrange("p (k n) -> p k n", k=kc), wbfb[j]
        i = wk[0] % NST
        wk[0] += 1
        stv = wst[i][:, 0:kc * n].rearrange("p (k n) -> p k n", k=kc)
        dma(stv, src.rearrange("(k p) n -> p k n", p=128), writes=[wstb[i]])
        if WC['on']:
            ci = WC['n']
            WC['n'] += 1
            bfv = wc_view[ci][:, 0:kc * n].rearrange("p (k n) -> p k n", k=kc)
            wk[1] += 1
            if wk[1] % 2 == 0:
                s.op('pool', lambda e: e.tensor_copy(out=bfv, in_=stv), reads=[wstb[i]], writes=[wc_buf[ci]])
            else:
                s.op('act', lambda e: e.activation(out=bfv, in_=stv, func=AF.Copy), reads=[wstb[i]], writes=[wc_buf[ci]])
            WC['map'][key] = (bfv, wc_buf[ci])
            return bfv, wc_buf[ci]
        j = wk[1] % NWB
        wk[1] += 1
        bfv = wbf[j][:, 0:kc * n].rearrange("p (k n) -> p k n", k=kc)
        if wk[1] % 2 == 0:
            s.op('pool', lambda e: e.tensor_copy(out=bfv, in_=stv), reads=[wstb[i]], writes=[wbfb[j]])
        else:
            s.op('act', lambda e: e.activation(out=bfv, in_=stv, func=AF.Copy), reads=[wstb[i]], writes=[wbfb[j]])
        if WD['mode'] == 'write' and key not in WD['map'] and len(WD['map']) < 112:
            di = len(WD['map'])
            WD['map'][key] = di
            dma(wcd_d[di][:, 0:kc * n], wbf[j][:, 0:kc * n], reads=[wbfb[j]], writes=[wcd_buf[di]])
        return bfv, wbfb[j]

    def mm(out, lhsT, rhs, start, stop, reads, writes):
        s.op('pe', lambda e: e.matmul(out, lhsT=lhsT, rhs=rhs, start=start, stop=stop), reads=reads, writes=writes)

    def P(l, off, n=1):
        return pp[:, l * PL + off:l * PL + off + n]

    O_N1, O_N2, O_BADA, O_QN, O_KN, O_SUB, O_CW, O_CB, O_LG, O_LB, O_LQ = 0, 8, 16, 64, 65, 66, 67, 191, 195, 199, 203

    dma(pp[:, :], pp_d, writes=[ppb])
    dma(cs[:, :], cs_d, writes=[csb])
    dma(matsf[:, :], mats_d, writes=[matb])
    dma(masks[:, :], masks_d, writes=[maskb])
    dma(cT[:, :], cT_d, writes=[ctb])
    for c in range(8):
        for b in range(2):
            dma(x[:, c, b * 512:(b + 1) * 512], xT_d[c * 128:(c + 1) * 128, b * 512:(b + 1) * 512], writes=[xb[c][b]])
    s.op('dve', lambda e: e.tensor_copy(out=matsb[:, :], in_=matsf[:, :]), reads=[matb], writes=[matb])
    s.op('dve', lambda e: e.memset(epsb[:, :], EPS), writes=[epsbb])
    s.op('act', lambda e: e.activation(out=scT[:, :], in_=cT[:, :], func=AF.Silu), reads=[ctb], writes=[ctb])

    try:
        BG = []

        def ada_tile(l, g4):
            md, wc_on = WD['mode'], WC['on']
            WD['mode'] = 'off'
            WC['on'] = False
            wv, wb_ = load_w(w_ada[l][:, g4 * 256:(g4 + 1) * 256])
            WD['mode'] = md
            WC['on'] = wc_on
            ps, pb = psum()
            for jj in range(2):
                for kc in range(8):
                    mm(ps[:, 2 * jj:2 * jj + 2], wv[:, kc, jj * 128:(jj + 1) * 128], scT[:, 2 * kc:2 * kc + 2],
                       kc == 0, kc == 7, [wb_, ctb], [pb])
            for g in range(2):
                s.op('dve', lambda e, g=g: e.tensor_tensor(
                    out=mod[:, l, 2 * g4:2 * g4 + 2, g], in0=ps[:, 0:4].rearrange("p (j g) -> p j g", g=2)[:, :, g],
                    in1=P(l, O_BADA + 2 * g4, 2), op=ALU.add), reads=[pb, ppb], writes=[modb])

        def bg_drain(n=1):
            for _ in range(min(n, len(BG))):
                BG.pop(0)()

        def ada_finish(l):
            for g in range(2):
                for (k_, on, osc, osh, og) in ((0, O_N1, 8, 0, 16), (3, O_N2, 32, 24, 40)):
                    s.op('dve', lambda e, l=l, g=g, k_=k_, on=on, osc=osc: e.scalar_tensor_tensor(
                        out=drv[:, l, g, k_, :], in0=mod[:, l, osc:osc + 8, g], scalar=1.0, in1=P(l, on, 8),
                        op0=ALU.add, op1=ALU.mult), reads=[modb, ppb], writes=[drvb])
                    s.op('dve', lambda e, l=l, g=g, k_=k_, osh=osh: e.tensor_copy(
                        out=drv[:, l, g, k_ + 1, :], in_=mod[:, l, osh:osh + 8, g]), reads=[modb], writes=[drvb])
                    s.op('dve', lambda e, l=l, g=g, k_=k_, og=og: e.tensor_copy(
                        out=drv[:, l, g, k_ + 2, :], in_=mod[:, l, og:og + 8, g]), reads=[modb], writes=[drvb])
            tv, tb_ = tmpf()
            for i in range(2):
                s.op('dve', lambda e, l=l, i=i, tv=tv: e.tensor_tensor(
                    out=tv[:, i * 64:(i + 1) * 64], in0=P(l, O_LQ + i * 128, 64), in1=P(l, O_LQ + i * 128 + 64, 64),
                    op=ALU.mult), reads=[ppb], writes=[tb_])
                s.op('dve', lambda e, l=l, i=i, tv=tv: e.reduce_sum(out=lam[:, l, 2 + i:3 + i], in_=tv[:, i * 64:(i + 1) * 64],
                                                                  axis=AX.X), reads=[tb_], writes=[lamb])
            s.op('act', lambda e, l=l: e.activation(out=lam[:, l, 2:4], in_=lam[:, l, 2:4], func=AF.Exp), reads=[lamb], writes=[lamb])
            lam_init = 0.8 - 0.6 * math.exp(-0.3 * l)
            s.op('dve', lambda e, l=l, li=lam_init: e.scalar_tensor_tensor(
                out=lam[:, l, 0:1], in0=lam[:, l, 2:3], scalar=li, in1=lam[:, l, 3:4], op0=ALU.add, op1=ALU.subtract),
                reads=[lamb], writes=[lamb])
            s.op('dve', lambda e, l=l: e.tensor_scalar(out=lam[:, l, 1:2], in0=lam[:, l, 0:1], scalar1=-1.0, scalar2=None,
                                                       op0=ALU.mult), reads=[lamb], writes=[lamb])


        for g4_ in range(24):
            ada_tile(0, g4_)
        ada_finish(0)
        for g4_ in range(24):
            BG.append(lambda g4_=g4_: ada_tile(1, g4_))

        def rstd_op(dst, src, scale, reads, dbuf):
            s.op('act', lambda e: e.activation(out=dst, in_=src, func=AF.Sqrt, bias=epsb[:, 0:1], scale=scale), reads=list(reads) + [epsbb], writes=[dbuf])
            s.op('dve', lambda e: e.reciprocal(out=dst, in_=dst), reads=[dbuf], writes=[dbuf])

        def rms_mod(l, ka):
            for b in list(BL):
                ps, pb = psum()
                for c in range(8):
                    tv, tb_ = tmpe()
                    s.op('act', lambda e, c=c, b=b, tv=tv: e.activation(out=tv[:, :], in_=x[:, c, b * 512:(b + 1) * 512],
                                                                      func=AF.Square), reads=[xb[c][b]], writes=[tb_])
                    mm(ps[:, :], ones_b, tv[:, :], c == 0, c == 7, [tb_, matb], [pb])
                rv, rb_ = LT[0], LTb[0]
                rstd_op(rv[:, :], ps[:, :], 1.0 / D, [pb], rb_)
                for c in range(8):
                    tv, tb_ = tmpf()
                    s.op('dve', lambda e, c=c, b=b, tv=tv, rv=rv: e.scalar_tensor_tensor(
                        out=tv[:, :], in0=x[:, c, b * 512:(b + 1) * 512], scalar=drv[:, l, b, ka, c:c + 1], in1=rv[:, :],
                        op0=ALU.mult, op1=ALU.mult), reads=[xb[c][b], drvb, rb_], writes=[tb_])
                    s.op('act', lambda e, c=c, b=b, tv=tv: e.activation(
                        out=h[:, c, b * 512:(b + 1) * 512], in_=tv[:, :], func=AF.Identity,
                        bias=drv[:, l, b, ka + 1, c:c + 1], scale=1.0), reads=[tb_, drvb], writes=[hb[c][b]])

        def hread(b):
            return [hb[c][b] for c in range(8)]

        def lin_fm(wsrc_fn, ncols, kcs, rhs_fn, rhs_reads_fn, consume, group=None, blocks=None):
            gcols = group or max(128, min(512, (2048 // kcs) // 128 * 128))
            for c0 in range(0, ncols, gcols):
                n = min(gcols, ncols - c0)
                bg_drain(1)
                wv, wb_ = load_w(wsrc_fn(c0, n))
                for jj in range(n // 128):
                    j = c0 // 128 + jj
                    for b in list(BL if blocks is None else blocks):
                        ps, pb = psum()
                        for kc in range(kcs):
                            mm(ps[:, :], wv[:, kc, jj * 128:(jj + 1) * 128], rhs_fn(kc, b), kc == 0, kc == kcs - 1,
                               [wb_] + rhs_reads_fn(b), [pb])
                        consume(j, b, ps, pb)

        def rope_store(ps, pb, dst, dstb, extra_reads=()):
            t1, t1b = tmpe()
            s.op('act', lambda e: e.activation(out=t1[:, :], in_=ps, func=AF.Copy), reads=[pb] + list(extra_reads), writes=[t1b])
            if ROPEL == 0:
                s.op('dve', lambda e: e.tensor_copy(out=dst, in_=t1[:, :]), reads=[t1b], writes=[dstb])
                return
            p2, p2b = psum()
            mm(p2[:, :], rot_b, t1[:, :], True, True, [t1b, matb], [p2b])
            if ROPEL == 1:
                s.op('dve', lambda e: e.tensor_copy(out=dst, in_=p2[:, :]), reads=[p2b], writes=[dstb])
                return
            t2, t2b = tmpf()
            if ROPEL == 3:
                s.op('dve', lambda e: e.tensor_copy(out=t2[:, :], in_=ps), reads=[pb, csb] + list(extra_reads), writes=[t2b])
            elif ROPEL == 4:
                s.op('dve', lambda e: e.tensor_tensor(out=t2[:, :], in0=cs[:, 512:1024], in1=cs[:, 0:512], op=ALU.mult),
                     reads=[pb, csb] + list(extra_reads), writes=[t2b])
            else:
                s.op('dve', lambda e: e.tensor_tensor(out=t2[:, :], in0=ps, in1=cs[:, 0:512], op=ALU.mult),
                     reads=[pb, csb] + list(extra_reads), writes=[t2b])
            if ROPEL in (2, 3, 4):
                s.op('dve', lambda e: e.tensor_copy(out=dst, in_=t2[:, :]), reads=[t2b, p2b], writes=[dstb])
                return
            t3, t3b = tmpf()
            s.op('dve', lambda e: e.tensor_tensor(out=t3[:, :], in0=p2[:, :], in1=cs[:, 512:1024], op=ALU.mult),
                 reads=[p2b, csb], writes=[t3b])
            s.op('dve', lambda e: e.tensor_tensor(out=dst, in0=t2[:, :], in1=t3[:, :], op=ALU.add),
                 reads=[t2b, t3b], writes=[dstb])

        def headnorm(ps, pb, gain_ap, l):
            sq, sqb = tmpf()
            s.op('act', lambda e: e.activation(out=sq[:, :], in_=ps, func=AF.Square), reads=[pb], writes=[sqb])
            p2, p2b = psum()
            mm(p2[:, :], bd_f, sq[:, :], True, True, [sqb, matb], [p2b])
            rv, rb_ = tmpf()
            rstd_op(rv[:, :], p2[:, :], 1.0 / 64, [p2b], rb_)
            ov, ob = tmpf()
            s.op('dve', lambda e: e.scalar_tensor_tensor(out=ov[:, :], in0=ps, scalar=gain_ap, in1=rv[:, :], op0=ALU.mult,
                                                         op1=ALU.mult), reads=[pb, rb_, ppb], writes=[ob])
            return ov, ob

        def attend(nq, q_ap, q_reads, ktiles, e_dim, out_fn):
            po, pob = psum_acc()
            pd, pdb = psum_acc()
            nk = len(ktiles)
            pend = None

            def pv(i, v_ap, rd, ev, eb):
                mm(po[0:e_dim, 0:nq], v_ap, ev[:, 0:nq], i == 0, i == nk - 1, [eb] + list(rd), [pob])
                mm(pd[:, 0:nq], ones_b, ev[:, 0:nq], i == 0, i == nk - 1, [eb, matb], [pdb])

            for i, (kT, v_ap, rd) in enumerate(ktiles):
                pss, psb_ = psum()
                mm(pss[:, 0:nq], kT, q_ap, True, True, list(rd) + list(q_reads), [psb_])
                ev, eb = tmpe()
                s.op('act', lambda e, pss=pss, ev=ev: e.activation(out=ev[:, 0:nq], in_=pss[:, 0:nq], func=AF.Exp, scale=0.125),
                     reads=[psb_], writes=[eb])
                if pend is not None:
                    pv(*pend)
                pend = (i, v_ap, rd, ev, eb)
            pv(*pend)
            rv, rb_ = tmpf()
            s.op('dve', lambda e: e.reciprocal(out=rv[:, 0:nq], in_=pd[:, 0:nq]), reads=[pdb], writes=[rb_])
            out_fn(po, pob, rv, rb_)

        chk('A')

        def kvproj(l):
            if not KVB:
                return
            hr = lambda kc, b: h[:, kc, b * 512:(b + 1) * 512]
            pass
            def c_dk(j, b, ps, pb):
                if b == 0:
                    s.op('act', lambda e: e.activation(out=dkT[:, j, 0:512], in_=ps[:, :], func=AF.Copy), reads=[pb], writes=[B['dkT0']])
                    tv, tb_ = tmpf()
                    s.op('dve', lambda e: e.tensor_copy(out=tv[:, :], in_=ps[:, :]), reads=[pb], writes=[tb_])
                    dma(okd_d[l][j * 128:(j + 1) * 128, :], tv[:, :], reads=[tb_])
                else:
                    rope_store(ps[:, :], pb, dkT[:, j, 512:1024], B['dkT1'])
            lin_fm(lambda c0, n: w_in[l][:, 512 + c0:512 + c0 + n], 512, 8, hr, hread, c_dk, group=256, blocks=KVB)

            pass
            def load_dup(col0):
                key = ('dup', l, col0)
                if WC['on'] and key in WC['map']:
                    return WC['map'][key]
                i = wk[0] % NST; wk[0] += 1
                j = wk[1] % NWB; wk[1] += 1
                stv = wst[i][:, 0:2048].rearrange("p (k n) -> p k n", k=8)
                for n in range(2):
                    for d_ in range(2):
                        dma(stv[:, :, (2 * n + d_) * 64:(2 * n + d_ + 1) * 64],
                            w_in[l][:, col0 + n * 64:col0 + (n + 1) * 64].rearrange("(k p) n -> p k n", p=128), writes=[wstb[i]])
                if WC['on']:
                    ci = WC['n']
                    WC['n'] += 1
                    bfv = wc_view[ci][:, 0:2048].rearrange("p (k n) -> p k n", k=8)
                    s.op('pool', lambda e: e.tensor_copy(out=bfv, in_=stv), reads=[wstb[i]], writes=[wc_buf[ci]])
                    WC['map'][key] = (bfv, wc_buf[ci])
                    return bfv, wc_buf[ci]
                bfv = wbf[j][:, 0:2048].rearrange("p (k n) -> p k n", k=8)
                s.op('pool', lambda e: e.tensor_copy(out=bfv, in_=stv), reads=[wstb[i]], writes=[wbfb[j]])
                return bfv, wbfb[j]

            wv, wb_ = load_dup(2048)
            for n in range(2):
                for b in list(KVB):
                    ps, pb = psum()
                    for kc in range(8):
                        mm(ps[:, :], wv[:, kc, n * 128:(n + 1) * 128], hr(kc, b), kc == 0, kc == 7, [wb_] + hread(b), [pb])
                    ov, ob = headnorm(ps[:, :], pb, P(l, O_KN), l)
                    if b == 0:
                        s.op('act', lambda e, n=n, ov=ov: e.activation(out=gkT[:, n, 0:512], in_=ov[:, :], func=AF.Copy),
                             reads=[ob], writes=[B['gkT0']])
                        dma(okg_d[l][n * 64:(n + 1) * 64, :], ov[0:64, :], reads=[ob])
                    else:
                        rope_store(ov[:, :], ob, gkT[:, n, 512:1024], B['gkT1'])

            pass
            for half in range(2):
                wv, wb_ = load_w(w_in[l][:, 1024 + half * 256:1024 + (half + 1) * 256])
                for t in range(8):
                    b = t // 4
                    if b not in KVB:
                        continue
                    ps, pb = psum()
                    for kc in range(8):
                        mm(ps[:, 0:256], h[:, kc, t * 128:(t + 1) * 128], wv[:, kc, :], kc == 0, kc == 7, [wb_] + hread(b), [pb])
                    s.op('act', lambda e, t=t, half=half, ps=ps: e.activation(out=Vd[:, t, half * 256:(half + 1) * 256], in_=ps[:, 0:256],
                                                                            func=AF.Copy), reads=[pb], writes=[B['Vd%d' % b]])
                    if b == 0:
                        tv, tb_ = tmpf()
                        s.op('dve', lambda e, tv=tv, ps=ps: e.tensor_copy(out=tv[:, 0:256], in_=ps[:, 0:256]), reads=[pb], writes=[tb_])
                        dma(ovd_d[l][t * 128:(t + 1) * 128, half * 256:(half + 1) * 256], tv[:, 0:256], reads=[tb_])
            pass
            wv, wb_ = load_dup(2176)
            for t in range(8):
                b = t // 4
                if b not in KVB:
                    continue
                ps, pb = psum()
                for kc in range(8):
                    mm(ps[:, 0:256], h[:, kc, t * 128:(t + 1) * 128], wv[:, kc, :], kc == 0, kc == 7, [wb_] + hread(b), [pb])
                s.op('act', lambda e, t=t, ps=ps: e.activation(out=Vg[:, t, :], in_=ps[:, 0:256], func=AF.Copy), reads=[pb],
                     writes=[B['Vg%d' % b]])
                if b == 0:
                    tv, tb_ = tmpf()
                    s.op('dve', lambda e, tv=tv, ps=ps: e.tensor_copy(
                        out=tv[:, 0:128].rearrange("p (n d) -> p n d", n=2),
                        in_=ps[:, 0:256].rearrange("p (n d) -> p n d", n=2)[:, :, 0:64]), reads=[pb], writes=[tb_])
                    dma(ovg_d[l][t * 128:(t + 1) * 128, :], tv[:, 0:128], reads=[tb_])


        def load_x(src, rd):
            for c in range(8):
                dma(x[:, c, 512:1024], src[c * 128:(c + 1) * 128, :], reads=rd, writes=[xb[c][1]])

        def prepass(l, slot):
            BL[:] = [1]
            KVB[:] = [1]
            if l == 0:
                load_x(xsT_d[slot], [])
            else:
                load_x(xs1_d[slot], [B['xs1_%d' % slot]])
            dma(cs[:, :], cs4_d[slot], writes=[csb])
            rms_mod(l, 0)
            kvproj(l)
            s.op('dve', lambda e: e.tensor_copy(out=hedge[:, :, 0:15], in_=h[:, :, 512:527]), reads=hread(1), writes=[hedgeb])
            s.op('dve', lambda e: e.tensor_copy(out=hedge[:, :, 15:30], in_=h[:, :, 1009:1024]), reads=hread(1), writes=[hedgeb])
            s.op('dve', lambda e: e.memset(edge[:, 0:128], 0.0), writes=[B['edge']])
            edge3 = edge[:, 0:120].rearrange("p (c n) -> p c n", c=4)
            sge, sgeb = LT[1], LTb[1]
            sge3 = sge[:, 0:120].rearrange("p (c n) -> p c n", c=4)
            for part, col0 in ((0, 2816), (1, 2304)):
                for c0 in (0, 256):
                    wv, wb_ = load_w(w_in[l][:, col0 + c0:col0 + c0 + 256])
                    for jj in range(2):
                        j = c0 // 128 + jj
                        ps, pb = psum()
                        for kc in range(8):
                            mm(ps[:, 0:30], wv[:, kc, jj * 128:(jj + 1) * 128], hedge[:, kc, :], kc == 0, kc == 7, [wb_, hedgeb], [pb])
                        if part == 0:
                            s.op('act', lambda e, j=j, ps=ps: e.activation(out=sge3[:, j, :], in_=ps[:, 0:30], func=AF.Sigmoid),
                                 reads=[pb], writes=[sgeb])
                        else:
                            s.op('dve', lambda e, j=j, ps=ps: e.tensor_tensor(out=edge3[:, j, :], in0=ps[:, 0:30], in1=sge3[:, j, :], op=ALU.mult),
                                 reads=[pb, sgeb], writes=[B['edge']])
            kb = B['kv%d_%d' % (l, slot)]
            rows = kv_d[l][slot * 128:(slot + 1) * 128, :]
            for c in range(4):
                dma(rows[:, c * 512:(c + 1) * 512], dkT[:, c, 512:1024], reads=[B['dkT1']], writes=[kb])
            for n in range(2):
                dma(rows[:, (4 + n) * 512:(5 + n) * 512], gkT[:, n, 512:1024], reads=[B['gkT1']], writes=[kb])
            for t in range(4):
                dma(rows[:, (6 + t) * 512:(7 + t) * 512], Vd[:, 4 + t, :], reads=[B['Vd1']], writes=[kb])
            for k in range(2):
                dma(rows[:, (10 + k) * 512:(11 + k) * 512].rearrange("p (a n) -> p a n", a=2), Vg[:, 4 + 2 * k:6 + 2 * k, :],
                    reads=[B['Vg1']], writes=[kb])
            dma(rows[:, 12 * 512:12 * 512 + 128], edge[:, 0:128], reads=[B['edge']], writes=[kb])

        def layer(l, slot, with_prompt):
            BL[:] = [0, 1] if with_prompt else [1]
            KVB[:] = [0] if with_prompt else []
            qs = lambda b: [B['dqT%d' % b], B['gqT%d' % b]]
            rms_mod(l, 0)
            hr = lambda kc, b: h[:, kc, b * 512:(b + 1) * 512]
            chk('A1_%d' % l)

            def c_dq(j, b, ps, pb):
                if b == 0 or NOROPE:
                    b0 = b * 512
                    s.op('act', lambda e: e.activation(out=dqT[:, j, b0:b0 + 512], in_=ps[:, :], func=AF.Copy), reads=[pb], writes=[B['dqT%d' % b]])
                elif b == 0:
                    s.op('act', lambda e: e.activation(out=dqT[:, j, 0:512], in_=ps[:, :], func=AF.Copy), reads=[pb], writes=[B['dqT0']])
                else:
                    rope_store(ps[:, :], pb, dqT[:, j, 512:1024], B['dqT1'])
            lin_fm(lambda c0, n: w_in[l][:, c0:c0 + n], 512, 8, hr, hread, c_dq, group=256)

            chk('A3')
            def c_gq(j, b, ps, pb):
                ov, ob = headnorm(ps[:, :], pb, P(l, O_QN), l)
                if b == 0:
                    s.op('act', lambda e: e.activation(out=gqT[:, j, 0:512], in_=ov[:, :], func=AF.Copy), reads=[ob], writes=[B['gqT0']])
                else:
                    rope_store(ov[:, :], ob, gqT[:, j, 512:1024], B['gqT1'])
            lin_fm(lambda c0, n: w_in[l][:, 1536 + c0:1536 + c0 + n], 512, 8, hr, hread, c_gq, group=256)

            kvproj(l)
            chk('A7')
            s.op('pool', lambda e: e.memset(upad[:, :, :], 0.0), writes=[B['upad']])
            SGB = Buf()
            Sched.alias([SGB], [B['daT'], B['gaT']])

            def c_cg(j, b, ps, pb):
                s.op('act', lambda e: e.activation(out=sgv[:, j, b * 512:(b + 1) * 512], in_=ps[:, :], func=AF.Sigmoid), reads=[pb], writes=[SGB])
            lin_fm(lambda c0, n: w_in[l][:, 2816 + c0:2816 + c0 + n], 512, 8, hr, hread, c_cg, group=256)
            useg = [(0, 15), (256, 301), (512, 587)]

            def c_ca(j, b, ps, pb):
                if b == 0:
                    for sq_ in range(2):
                        t0, u0 = useg[sq_]
                        s.op('dve', lambda e, t0=t0, u0=u0: e.tensor_tensor(out=upad[:, j, u0:u0 + 256], in0=ps[:, t0:t0 + 256],
                                                                           in1=sgv[:, j, t0:t0 + 256], op=ALU.mult),
                             reads=[pb, SGB], writes=[B['upad']])
                else:
                    s.op('dve', lambda e: e.tensor_tensor(out=upad[:, j, 587:587 + 512], in0=ps[:, :], in1=sgv[:, j, 512:1024], op=ALU.mult),
                         reads=[pb, SGB], writes=[B['upad']])
            lin_fm(lambda c0, n: w_in[l][:, 2304 + c0:2304 + c0 + n], 512, 8, hr, hread, c_ca, group=256)

            Sched.alias([B['daT'], B['gaT']], [SGB])
            chk('B%d' % l)
            kvv = kv_d[l].rearrange("(j p) f -> p j f", p=128)
            kvr = [B['kv%d_%d' % (l, i_)] for i_ in range(4)]

            def gather(u, n, dst, dstb, c0=0):
                dma(dst, kvv[:, :, u * 512 + c0:u * 512 + c0 + n], reads=kvr, writes=[dstb])

            Sched.alias([B['gK'], B['gV'], B['gA']], [B['xtmp']])
            sl, sr = (slot - 1) % 4, (slot + 1) % 4
            dma(gA[:, 0, 0:128], kv_d[l][sl * 128:(sl + 1) * 128, 12 * 512:12 * 512 + 128], reads=kvr, writes=[B['gA']])
            dma(gA[:, 1, 0:128], kv_d[l][sr * 128:(sr + 1) * 128, 12 * 512:12 * 512 + 128], reads=kvr, writes=[B['gA']])
            hal = gA[:, 0:2, 0:120].rearrange("p j (c n) -> p j c n", c=4)
            s.op('dve', lambda e: e.scalar_tensor_tensor(out=upad[:, :, 572:587], in0=hal[:, 0, :, 15:30], scalar=masks[:, slot:slot + 1],
                                                         in1=upad[:, :, 572:587], op0=ALU.mult, op1=ALU.add),
                 reads=[B['gA'], maskb, B['upad']], writes=[B['upad']])
            s.op('dve', lambda e: e.scalar_tensor_tensor(out=upad[:, :, 1099:1114], in0=hal[:, 1, :, 0:15], scalar=masks[:, 4 + slot:5 + slot],
                                                         in1=upad[:, :, 1099:1114], op0=ALU.mult, op1=ALU.add),
                 reads=[B['gA'], maskb, B['upad']], writes=[B['upad']])
            NV = 1084
            conv_ops = []
            for c in range(4):
                for k in range(31):
                    if k == 0:
                        conv_ops.append(lambda c=c: s.op('dve', lambda e: e.tensor_scalar(out=cacc[:, 0:NV], in0=upad[:, c, 0:NV], scalar1=P(l, O_CW + c * 31),
                                                                                          scalar2=P(l, O_CB + c), op0=ALU.mult, op1=ALU.add),
                                                         reads=[B['upad'], ppb], writes=[B['cacc']]))
                    else:
                        conv_ops.append(lambda c=c, k=k: s.op('dve', lambda e: e.scalar_tensor_tensor(out=cacc[:, 0:NV], in0=upad[:, c, k:k + NV],
                                                                                                      scalar=P(l, O_CW + c * 31 + k), in1=cacc[:, 0:NV],
                                                                                                      op0=ALU.mult, op1=ALU.add),
                                                              reads=[B['upad'], ppb, B['cacc']], writes=[B['cacc']]))
                conv_ops.append(lambda c=c: s.op('dve', lambda e: e.tensor_copy(out=upad[:, c, 15:15 + NV], in_=cacc[:, 0:NV]),
                                                 reads=[B['cacc']], writes=[B['upad']]))
            n_calls = [(32 if with_prompt else 0) + 16]

            def drain(final=False):
                n = len(conv_ops) if final else -(-len(conv_ops) // max(1, n_calls[0]))
                n_calls[0] -= 1
                for _ in range(min(n, len(conv_ops))):
                    conv_ops.pop(0)()

            def diff_out(l, h_, tok0, nq):
                st = {}

                def fn(comp):
                    def out_fn(po, pob, rv, rb_):
                        tv, tb_ = (LT[2], LTb[2]) if comp == 0 else tmpf()
                        s.op('dve', lambda e: e.tensor_tensor(out=tv[:, 0:nq], in0=po[:, 0:nq], in1=rv[:, 0:nq], op=ALU.mult),
                             reads=[pob, rb_], writes=[tb_])
                        st[comp] = (tv, tb_)
                        if comp == 1:
                            t0v, t0b = st[0]
                            dv_, db_ = tmpf()
                            s.op('dve', lambda e: e.scalar_tensor_tensor(out=dv_[:, 0:nq], in0=tv[:, 0:nq], scalar=lam[:, l, 1:2],
                                                                         in1=t0v[:, 0:nq], op0=ALU.mult, op1=ALU.add),
                                 reads=[tb_, t0b, lamb], writes=[db_])
                            sq, sqb = tmpf()
                            s.op('act', lambda e: e.activation(out=sq[:, 0:nq], in_=dv_[:, 0:nq], func=AF.Square), reads=[db_], writes=[sqb])
                            p2, p2b = psum()
                            mm(p2[:, 0:nq], ones_f, sq[:, 0:nq], True, True, [sqb, matb], [p2b])
                            r2, r2b = tmpf()
                            rstd_op(r2[:, 0:nq], p2[:, 0:nq], 1.0 / 128, [p2b], r2b)
                            s.op('dve', lambda e: e.scalar_tensor_tensor(out=dv_[:, 0:nq], in0=dv_[:, 0:nq], scalar=P(l, O_SUB),
                                                                         in1=r2[:, 0:nq], op0=ALU.mult, op1=ALU.mult),
                                 reads=[db_, r2b, ppb], writes=[db_])
                            li = 0.8 - 0.6 * math.exp(-0.3 * l)
                            s.op('act', lambda e: e.activation(out=daT[:, h_, tok0:tok0 + nq], in_=dv_[:, 0:nq], func=AF.Identity,
                                                               scale=1.0 - li), reads=[db_], writes=[B['daT']])
                    return out_fn
                return fn

            def gqa_out(c, par, tok0, nq):
                def out_fn(po, pob, rv, rb_):
                    s.op('dve', lambda e: e.tensor_tensor(out=gaT[par * 64:(par + 1) * 64, c, tok0:tok0 + nq],
                                                          in0=po[par * 64:(par + 1) * 64, 0:nq], in1=rv[par * 64:(par + 1) * 64, 0:nq],
                                                          op=ALU.mult), reads=[pob, rb_], writes=[B['gaT']])
                return out_fn

            for sq_ in (range(2) if with_prompt else ()):
                tok0 = sq_ * 256
                for h_ in range(4):
                    of = diff_out(l, h_, tok0, 256)
                    for comp in range(2):
                        r0 = comp * 64
                        kts = [(dkT[r0:r0 + 64, h_, tok0 + kt * 128:tok0 + (kt + 1) * 128],
                                Vd[:, sq_ * 2 + kt, h_ * 128:(h_ + 1) * 128], [B['dkT0'], B['Vd0']]) for kt in range(2)]
                        attend(256, dqT[r0:r0 + 64, h_, tok0:tok0 + 256], [B['dqT0']], kts, 128, of(comp))
                        drain()
                for n in range(2):
                    for g in range(4):
                        c = n * 2 + g // 2
                        par = g % 2
                        r0 = par * 64
                        kts = [(gkT[r0:r0 + 64, n, tok0 + kt * 128:tok0 + (kt + 1) * 128],
                                Vg[:, sq_ * 2 + kt, n * 128:(n + 1) * 128], [B['gkT0'], B['Vg0']]) for kt in range(2)]
                        attend(256, gqT[r0:r0 + 64, c, tok0:tok0 + 256], [B['gqT0']], kts, 128, gqa_out(c, par, tok0, 256))
                        drain()

            chk('C%d' % l)
            chk('D%d' % l)
            for h_ in range(4):
                gather(h_, 512, gK[:, 0:2048].rearrange("p (j n) -> p j n", j=4), B['gK'])
                tv, tb_ = tmpf()
                dma(tv[:, 0:256], ckd_d[l][h_ * 128:(h_ + 1) * 128, :], writes=[tb_])
                s.op('pool', lambda e, tv=tv: e.tensor_copy(out=gK[:, 2048:2304], in_=tv[:, 0:256]), reads=[tb_], writes=[B['gK']])
                for t in range(4):
                    gather(6 + t, 128, gV[:, 0:16, :].rearrange("p (j t) n -> p j t n", t=4)[:, :, t, :], B['gV'], c0=h_ * 128)
                for t in range(2):
                    tv, tb_ = tmpf()
                    dma(tv[:, 0:128], cvd_d[l][t * 128:(t + 1) * 128, h_ * 128:(h_ + 1) * 128], writes=[tb_])
                    s.op('pool', lambda e, tv=tv, t=t: e.tensor_copy(out=gV[:, 16 + t, :], in_=tv[:, 0:128]), reads=[tb_], writes=[B['gV']])
                of = diff_out(l, h_, 512, 512)
                for comp in range(2):
                    r0 = comp * 64
                    kts = [(gK[r0:r0 + 64, kt * 128:(kt + 1) * 128], gV[:, kt, :], [B['gK'], B['gV']]) for kt in range(18)]
                    attend(512, dqT[r0:r0 + 64, h_, 512:1024], [B['dqT1']], kts, 128, of(comp))
                    drain()
            for n in range(2):
                gather(4 + n, 512, gK[:, 0:2048].rearrange("p (j n) -> p j n", j=4), B['gK'])
                tv, tb_ = tmpf()
                dma(tv[:, 0:256], ckg_d[l][n], writes=[tb_])
                s.op('pool', lambda e, tv=tv: e.tensor_copy(out=gK[:, 2048:2304], in_=tv[:, 0:256]), reads=[tb_], writes=[B['gK']])
                for t in range(4):
                    gather(10 + t // 2, 128, gV[:, 0:16, :].rearrange("p (j t) n -> p j t n", t=4)[:, :, t, :], B['gV'],
                           c0=(t % 2) * 256 + n * 128)
                for t in range(2):
                    tv, tb_ = tmpf()
                    dma(tv[:, 0:128], cvg_d[l][t * 128:(t + 1) * 128, n * 128:(n + 1) * 128], writes=[tb_])
                    s.op('pool', lambda e, tv=tv, t=t: e.tensor_copy(out=gV[:, 16 + t, :], in_=tv[:, 0:128]), reads=[tb_], writes=[B['gV']])
                for g in range(4):
                    c = n * 2 + g // 2
                    par = g % 2
                    r0 = par * 64
                    kts = [(gK[r0:r0 + 64, kt * 128:(kt + 1) * 128], gV[:, kt, :], [B['gK'], B['gV']]) for kt in range(18)]
                    attend(512, gqT[r0:r0 + 64, c, 512:1024], [B['gqT1']], kts, 128, gqa_out(c, par, 512, 512))
                    drain()

            drain(final=True)
            Sched.alias([B['ucT']], [B['gK'], B['gV'], B['gA']])
            for (t0, u0, n) in (((0, 15, 256), (256, 301, 256), (512, 587, 512)) if with_prompt else ((512, 587, 512),)):
                pm, pmb = psum()
                pq, pqb = psum()
                for c in range(4):
                    mm(pm[:, 0:n], ones_f, upad[:, c, u0:u0 + n], c == 0, c == 3, [B['upad'], matb], [pmb])
                for c in range(4):
                    sq, sqb = tmpf()
                    s.op('act', lambda e, c=c, sq=sq, u0=u0, n=n: e.activation(out=sq[:, 0:n], in_=upad[:, c, u0:u0 + n], func=AF.Square),
                         reads=[B['upad']], writes=[sqb])
                    mm(pq[:, 0:n], ones_f, sq[:, 0:n], c == 0, c == 3, [sqb, matb], [pqb])
                mu, mub = LT[0], LTb[0]
                s.op('dve', lambda e, mu=mu, pm=pm, n=n: e.tensor_scalar(out=mu[:, 0:n], in0=pm[:, 0:n], scalar1=1.0 / 512, scalar2=None,
                                                                         op0=ALU.mult), reads=[pmb], writes=[mub])
                var, vb = LT[1], LTb[1]
                s.op('dve', lambda e, var=var, mu=mu, n=n: e.tensor_tensor(out=var[:, 0:n], in0=mu[:, 0:n], in1=mu[:, 0:n], op=ALU.mult),
                     reads=[mub], writes=[vb])
                s.op('dve', lambda e, var=var, pq=pq, n=n: e.scalar_tensor_tensor(out=var[:, 0:n], in0=pq[:, 0:n], scalar=1.0 / 512,
                                                                                  in1=var[:, 0:n], op0=ALU.mult, op1=ALU.subtract),
                     reads=[pqb, vb], writes=[vb])
                rstd_op(var[:, 0:n], var[:, 0:n], 1.0, [vb], vb)
                for c in range(4):
                    tv, tb_ = tmpf()
                    s.op('dve', lambda e, c=c, tv=tv, mu=mu, u0=u0, n=n: e.tensor_tensor(out=tv[:, 0:n], in0=upad[:, c, u0:u0 + n],
                                                                                       in1=mu[:, 0:n], op=ALU.subtract),
                         reads=[B['upad'], mub], writes=[tb_])
                    s.op('dve', lambda e, c=c, tv=tv, var=var, n=n: e.scalar_tensor_tensor(out=tv[:, 0:n], in0=tv[:, 0:n], scalar=P(l, O_LG + c),
                                                                                         in1=var[:, 0:n], op0=ALU.mult, op1=ALU.mult),
                         reads=[tb_, vb, ppb], writes=[tb_])
                    s.op('act', lambda e, c=c, tv=tv, t0=t0, n=n: e.activation(out=ucT[:, c, t0:t0 + n], in_=tv[:, 0:n], func=AF.Silu,
                                                                             bias=P(l, O_LB + c), scale=1.0),
                         reads=[tb_, ppb], writes=[B['ucT']])

            chk('E%d' % l)
            Sched.alias([B['merged']], [B['dqT0'], B['dqT1'], B['gqT0'], B['gqT1']])
            for j in range(8):
                for i, (wsrc, act, ab) in enumerate(((w_diff_o, daT, B['daT']), (w_gqa_o, gaT, B['gaT']), (w_conv_o, ucT, B['ucT']))):
                    wv, wb_ = load_w(wsrc[l][:, j * 128:(j + 1) * 128])
                    gw, gwb = load_w(w_in[l][:, 3328 + i * 1024 + j * 128:3328 + i * 1024 + (j + 1) * 128])
                    for b in list(BL):
                        acc, accb = LT[b], LTb[b]
                        pbr, pbrb = psum()
                        for kc in range(4):
                            mm(pbr[:, :], wv[:, kc, :], act[:, kc, b * 512:(b + 1) * 512], kc == 0, kc == 3, [wb_, ab], [pbrb])
                        pg, pgb = psum()
                        for kc in range(8):
                            mm(pg[:, :], gw[:, kc, :], hr(kc, b), kc == 0, kc == 7, [gwb] + hread(b), [pgb])
                        gv_, gb_ = tmpf()
                        s.op('act', lambda e, gv_=gv_, pg=pg: e.activation(out=gv_[:, :], in_=pg[:, :], func=AF.Sigmoid), reads=[pgb], writes=[gb_])
                        if i == 0:
                            s.op('dve', lambda e, acc=acc, pbr=pbr, gv_=gv_: e.tensor_tensor(out=acc[:, :], in0=pbr[:, :], in1=gv_[:, :], op=ALU.mult),
                                 reads=[pbrb, gb_], writes=[accb])
                        else:
                            s.op('dve', lambda e, pbr=pbr, gv_=gv_: e.tensor_tensor(out=gv_[:, :], in0=pbr[:, :], in1=gv_[:, :], op=ALU.mult),
                                 reads=[pbrb, gb_], writes=[gb_])
                            if i == 1:
                                s.op('dve', lambda e, acc=acc, gv_=gv_: e.tensor_tensor(out=acc[:, :], in0=acc[:, :], in1=gv_[:, :], op=ALU.add),
                                     reads=[accb, gb_], writes=[accb])
                            else:
                                s.op('dve', lambda e, acc=acc, gv_=gv_, j=j, b=b: e.tensor_tensor(out=merged[:, j, b * 512:(b + 1) * 512], in0=acc[:, :],
                                                                                             in1=gv_[:, :], op=ALU.add),
                                     reads=[accb, gb_], writes=[B['merged']])

            def c_wo(j, b, ps, pb):
                s.op('dve', lambda e: e.scalar_tensor_tensor(out=x[:, j, b * 512:(b + 1) * 512], in0=ps[:, :], scalar=drv[:, l, b, 2, j:j + 1],
                                                             in1=x[:, j, b * 512:(b + 1) * 512], op0=ALU.mult, op1=ALU.add),
                     reads=[pb, drvb, xb[j][b]], writes=[xb[j][b]])
            lin_fm(lambda c0, n: w_o[l][:, c0:c0 + n], 1024, 8, lambda kc, b: merged[:, kc, b * 512:(b + 1) * 512], lambda b: [B['merged']], c_wo,
                   group=256)

            chk('F%d' % l)
            rms_mod(l, 3)
            Sched.alias([B['hid0'], B['hid1']], [B[k] for k in ['merged', 'dkT0', 'dkT1', 'gkT0', 'gkT1', 'Vd0', 'Vd1', 'Vg0', 'Vg1', 'upad',
                                                                'edge', 'xtmp', 'gK', 'gV', 'gA', 'daT', 'gaT', 'ucT', 'cacc']])

            def c_m1(j, b, ps, pb):
                tv, tb_ = tmpf()
                s.op('act', lambda e: e.activation(out=tv[:, :], in_=ps[:, :], func=AF.Relu), reads=[pb], writes=[tb_])
                s.op('dve', lambda e: e.tensor_tensor(out=hid[:, j, b * 512:(b + 1) * 512], in0=tv[:, :], in1=tv[:, :], op=ALU.mult),
                     reads=[tb_], writes=[B['hid%d' % b]])
            lin_fm(lambda c0, n: w_mlp1[l][:, c0:c0 + n], 4096, 8, hr, hread, c_m1, group=256)
            for j in range(8):
                wts = [load_w(w_mlp2[l][kh * 2048:(kh + 1) * 2048, j * 128:(j + 1) * 128]) for kh in range(2)]
                for b in list(BL):
                    ps, pb = psum()
                    for kh in range(2):
                        wv, wb_ = wts[kh]
                        for kc in range(16):
                            kk_ = kh * 16 + kc
                            mm(ps[:, :], wv[:, kc, :], hid[:, kk_, b * 512:(b + 1) * 512], kk_ == 0, kk_ == 31, [wb_, B['hid%d' % b]], [pb])
                    s.op('dve', lambda e, j=j, b=b, ps=ps: e.scalar_tensor_tensor(out=x[:, j, b * 512:(b + 1) * 512], in0=ps[:, :],
                                                                                scalar=drv[:, l, b, 5, j:j + 1], in1=x[:, j, b * 512:(b + 1) * 512],
                                                                                op0=ALU.mult, op1=ALU.add),
                         reads=[pb, drvb, xb[j][b]], writes=[xb[j][b]])
            chk('G%d' % l)
            allr = [B[k] for k in ['dqT0', 'dqT1', 'gqT0', 'gqT1', 'dkT0', 'dkT1', 'gkT0', 'gkT1', 'Vd0', 'Vd1', 'Vg0', 'Vg1', 'upad', 'edge',
                                   'xtmp', 'daT', 'gaT', 'cacc']]
            Sched.alias(allr, [B['hid0'], B['hid1']])

        for l_ in range(L):
            if l_ == 1:
                bg_drain(100)
                ada_finish(1)
            shared = [B[k] for k in ['upad', 'xtmp', 'gK', 'gV', 'gA', 'daT', 'gaT', 'ucT', 'cacc', 'hid0', 'hid1']]
            Sched.alias(wc_buf, shared)
            WC['on'], WC['map'], WC['n'] = True, {}, 0
            WD['mode'] = 'off'
            for i_ in ((0, 1, 2, 3) if l_ == 0 else (1, 2, 3, 0)):
                prepass(l_, i_)
            WC['on'] = False
            Sched.alias(shared, wc_buf)
            if l_ == 0:
                for i_ in (3, 2, 1, 0):
                    WD['mode'] = 'write' if i_ == 3 else 'read'
                    load_x(xsT_d[i_], [])
                    dma(cs[:, :], cs4_d[i_], writes=[csb])
                    layer(0, i_, i_ == 3)
                    for c in range(8):
                        dma(xs1_d[i_][c * 128:(c + 1) * 128, :], x[:, c, 512:1024], reads=[xb[c][1]], writes=[B['xs1_%d' % i_]])
            else:
                WD['mode'] = 'off'
                layer(1, 0, True)

        for b in range(2):
            ps, pb = psum()
            for c in range(8):
                tv, tb_ = tmpf()
                s.op('act', lambda e, c=c, b=b, tv=tv: e.activation(out=tv[:, :], in_=x[:, c, b * 512:(b + 1) * 512], func=AF.Square),
                     reads=[xb[c][b]], writes=[tb_])
                mm(ps[:, :], ones_f, tv[:, :], c == 0, c == 7, [tb_, matb], [pb])
            rv, rb_ = LT[0], LTb[0]
            rstd_op(rv[:, :], ps[:, :], 1.0 / D, [pb], rb_)
            for c in range(8):
                tv, tb_ = tmpf()
                s.op('dve', lambda e, c=c, b=b, tv=tv, rv=rv: e.scalar_tensor_tensor(
                    out=tv[:, :], in0=x[:, c, b * 512:(b + 1) * 512], scalar=pp[:, 2 * PL + c:2 * PL + c + 1], in1=rv[:, :],
                    op0=ALU.mult, op1=ALU.mult), reads=[xb[c][b], ppb, rb_], writes=[tb_])
                dma(yT_d[c * 128:(c + 1) * 128, b * 512:(b + 1) * 512], tv[:, :], reads=[tb_])

    except StopBuild:
        pass
    s.emit()
    return nc


def _rope_tables(pos):
    gw = 64
    row = (pos // gw).astype(np.float32)
    col = (pos % gw).astype(np.float32)
    freqs = (10000.0 ** (-np.arange(0, 32, 2, dtype=np.float32) / 32)).astype(np.float32)
    ang_r = row[:, None] * freqs[None, :]
    ang_c = col[:, None] * freqs[None, :]
    ang = np.concatenate([ang_r, ang_r, ang_c, ang_c], -1).astype(np.float32)
    return np.cos(ang).astype(np.float32), np.sin(ang).astype(np.float32)


def kernel(x_prompt, x_sample, cache_diff_k, cache_diff_v, cache_gqa_k, cache_gqa_v, c, c_ctx,
           w_ada, b_ada, norm1, norm2, w_in, diff_lq1, diff_lk1, diff_lq2, diff_lk2, diff_subln,
           w_diff_o, gqa_q_norm, gqa_k_norm, w_gqa_o, conv_dw, conv_dw_b, conv_ln_g, conv_ln_b,
           w_conv_o, w_o, w_mlp1, w_mlp2, final_norm):
    f = lambda a: np.ascontiguousarray(np.asarray(a, dtype=np.float32))
    x_prompt, x_sample = f(x_prompt), f(x_sample)
    cdk, cdv, cgk, cgv = f(cache_diff_k), f(cache_diff_v), f(cache_gqa_k), f(cache_gqa_v)
    c, c_ctx = f(c), f(c_ctx)
    pp = np.zeros((128, NPP), np.float32)
    fm = lambda v, n: f(v).reshape(n, 128).T
    for l in range(L):
        o = l * PL
        pp[:, o:o + 8] = fm(norm1[l], 8)
        pp[:, o + 8:o + 16] = fm(norm2[l], 8)
        pp[:, o + 16:o + 64] = fm(b_ada[l], 48)
        pp[:, o + 64] = np.tile(f(gqa_q_norm[l]), 2)
        pp[:, o + 65] = np.tile(f(gqa_k_norm[l]), 2)
        pp[:, o + 66] = f(diff_subln[l])
        cw = f(conv_dw[l])
        pp[:, o + 67:o + 191] = cw.T.reshape(4, 128, 31).transpose(1, 0, 2).reshape(128, 124)
        pp[:, o + 191:o + 195] = fm(conv_dw_b[l], 4)
        pp[:, o + 195:o + 199] = fm(conv_ln_g[l], 4)
        pp[:, o + 199:o + 203] = fm(conv_ln_b[l], 4)
        for i, v in enumerate((diff_lq1, diff_lk1, diff_lq2, diff_lk2)):
            pp[:, o + 203 + i * 64:o + 203 + (i + 1) * 64] = np.broadcast_to(f(v[l])[None, :], (128, 64))
    pp[:, 2 * PL:2 * PL + 8] = fm(final_norm, 8)
    mats = np.zeros((128, 384), np.float32)
    for b0 in (0, 64):
        for i in range(16):
            mats[b0 + 16 + i, b0 + i] = -1.0
            mats[b0 + i, b0 + 16 + i] = 1.0
            mats[b0 + 48 + i, b0 + 32 + i] = -1.0
            mats[b0 + 32 + i, b0 + 48 + i] = 1.0
        mats[b0:b0 + 64, 128 + b0:128 + b0 + 64] = 1.0
    mats[:, 256:384] = 1.0
    ws = {k: f(v) for k, v in dict(w_ada=w_ada, w_in=w_in, w_diff_o=w_diff_o, w_gqa_o=w_gqa_o, w_conv_o=w_conv_o, w_o=w_o,
                                   w_mlp1=w_mlp1, w_mlp2=w_mlp2).items()}
    in_maps = []
    for r in range(8):
        sq, q = r // 4, r % 4
        xt = np.concatenate([x_prompt[2 * r], x_prompt[2 * r + 1], x_sample[sq, q * 512:(q + 1) * 512]], 0)
        cT = np.stack([c_ctx.reshape(8, 128).T, c[sq].reshape(8, 128).T], -1).reshape(128, 16)
        cos, sin = _rope_tables(np.arange(q * 512, (q + 1) * 512))
        cs = np.concatenate([np.tile(cos.T, (2, 1)), np.tile(sin.T, (2, 1))], 1)
        masks = np.zeros((128, 18), np.float32)
        masks[:, r] = 1.0
        masks[:, 8 + sq] = 1.0
        if q > 0:
            masks[:, 10 + q - 1] = 1.0
        if q < 3:
            masks[:, 14 + q + 1] = 1.0
        ckd = cdk[sq].reshape(L, 256, 512).transpose(0, 2, 1)
        ckg = cgk[sq].transpose(0, 2, 3, 1)
        ckg = np.concatenate([ckg, ckg], 2)
        cvd = cdv[sq].reshape(L, 256, 512)
        cvg = np.concatenate([cgv[sq], cgv[sq]], -1).reshape(L, 256, 256)
        xs_l, cs_l = [], []
        masks[:] = 0.0
        for i in range(4):
            g_ = (q + i) % 4
            xs_l.append(x_sample[sq, g_ * 512:(g_ + 1) * 512].T)
            co, si = _rope_tables(np.arange(g_ * 512, (g_ + 1) * 512))
            cs_l.append(np.concatenate([np.tile(co.T, (2, 1)), np.tile(si.T, (2, 1))], 1))
            masks[:, i] = 1.0 if g_ > 0 else 0.0
            masks[:, 4 + i] = 1.0 if g_ < 3 else 0.0
        m = dict(xT=f(xt.T), cT=f(cT), pp=pp, cs=f(cs), mats=mats, masks=masks, ckd=f(ckd), ckg=f(ckg), cvd=f(cvd), cvg=f(cvg),
                 xsT=f(np.stack(xs_l)), cs4=f(np.stack(cs_l)))
        m.update(ws)
        in_maps.append(m)
    nc = build()
    res = run_bass_kernel_spmd(nc, in_maps, core_ids=list(range(8)))
    y_prompt = np.zeros((16, 256, D), np.float32)
    y_sample = np.zeros((2, 2048, D), np.float32)
    ndk = np.zeros((16, L, 256, 4, 2, 64), np.float32)
    ndv = np.zeros((16, L, 256, 4, 128), np.float32)
    ngk = np.zeros((16, L, 256, 2, 64), np.float32)
    ngv = np.zeros((16, L, 256, 2, 64), np.float32)
    for r in range(8):
        o = res.results[r]
        sq, q = r // 4, r % 4
        y = np.asarray(o["yT"]).T
        y_prompt[2 * r] = y[0:256]
        y_prompt[2 * r + 1] = y[256:512]
        y_sample[sq, q * 512:(q + 1) * 512] = y[512:1024]
        for i in range(2):
            ndk[2 * r + i] = np.asarray(o["okd"])[:, :, i * 256:(i + 1) * 256].transpose(0, 2, 1).reshape(L, 256, 4, 2, 64)
            ngk[2 * r + i] = np.asarray(o["okg"])[:, :, i * 256:(i + 1) * 256].transpose(0, 2, 1).reshape(L, 256, 2, 64)
            ndv[2 * r + i] = np.asarray(o["ovd"])[:, i * 256:(i + 1) * 256, :].reshape(L, 256, 4, 128)
            ngv[2 * r + i] = np.asarray(o["ovg"])[:, i * 256:(i + 1) * 256, :].reshape(L, 256, 2, 64)
    return (y_prompt, y_sample, ndk, ndv, ngk, ngv)
```

```python
import math
import numpy as np
import concourse.bass as bass
import concourse.mybir as mybir
from concourse.bass_utils import run_bass_kernel_spmd

F32 = mybir.dt.float32
BF16 = mybir.dt.bfloat16
ALU = mybir.AluOpType
AF = mybir.ActivationFunctionType
AX = mybir.AxisListType

ENGS = ['pe', 'act', 'dve', 'pool', 'sp']
D = 1024
L = 2
EPS = 1e-6
NU = 13
FROW = 12 * 512 + 128
PL = 459
NPP = 2 * PL + 8
NOCOLL = False
DBG = {}
NOROPE = False
ROPEL = 9
STOP = None


class StopBuild(Exception):
    pass


def chk(name):
    if STOP == name:
        raise StopBuild()


class Buf:
    __slots__ = ('name', 'w', 'r', 'excl')

    def __init__(self, name='', excl=False):
        self.name = name
        self.w = []
        self.r = []
        self.excl = excl


class Op:
    __slots__ = ('eng', 'fn', 'deps', 'dma', 'sig', 'sem', 'val', 'guard')

    def __init__(self, eng, fn, dma):
        self.eng = eng
        self.fn = fn
        self.dma = dma
        self.deps = []
        self.sig = False
        self.sem = None
        self.val = None
        self.guard = None


class Sched:
    NDMA = {'sp': 24, 'pool': 2, 'act': 2}

    def __init__(self, nc):
        self.nc = nc
        self.ops = {e: [] for e in ENGS}
        self.alldma = []

    def op(self, eng, fn, reads=(), writes=(), dma=False):
        o = Op(eng, fn, dma)
        deps = {}

        def add(d, raw):
            if d is o:
                return
            if eng == 'pe' and d.eng == 'pe' and (not dma) and (not d.dma):
                return
            deps[id(d)] = d

        for b in reads:
            for d in b.w:
                add(d, True)
            if b.excl:
                for d in b.r:
                    if d.eng != eng:
                        add(d, True)
        for b in writes:
            for d in b.w:
                add(d, False)
            for d in b.r:
                add(d, False)
        o.deps = list(deps.values())
        for d in o.deps:
            d.sig = True
        for b in writes:
            b.w = [o]
            b.r = []
        for b in reads:
            if b in writes:
                continue
            if not dma:
                b.r = [x for x in b.r if x.dma or x.eng != eng]
            b.r.append(o)
        self.ops[eng].append(o)
        if dma:
            self.alldma.append(o)
        return o

    @staticmethod
    def alias(dst, src):
        w = {}
        r = {}
        for s_ in src:
            for d in s_.w:
                w[id(d)] = d
            for d in s_.r:
                r[id(d)] = d
        for d in dst:
            d.w = list(w.values())
            d.r = list(r.values())

    def emit(self):
        nc = self.nc
        esem = {e: nc.alloc_semaphore('s_' + e) for e in ENGS}
        dsem = {e: [nc.alloc_semaphore('d_%s%d' % (e, i)) for i in range(n)] for e, n in self.NDMA.items()}
        dcount = {e: [0] * n for e, n in self.NDMA.items()}
        kk = {e: 0 for e in self.NDMA}
        for o in self.alldma:
            e = o.eng
            j = kk[e] % self.NDMA[e]
            kk[e] += 1
            inc = 16 if o.dma is True else o.dma
            dcount[e][j] += inc
            o.sem = dsem[e][j]
            o.val = dcount[e][j]
            o.guard = (o.sem, o.val - inc) if o.val > inc else None
        for e in ENGS:
            c = 0
            for o in self.ops[e]:
                if o.dma:
                    continue
                if o.sig:
                    c += 1
                    o.sem = esem[e]
                    o.val = c
        final = []
        for e in self.NDMA:
            for j in range(self.NDMA[e]):
                if dcount[e][j]:
                    final.append((dsem[e][j], dcount[e][j]))

        def run(e):
            def body(eng):
                seen = {}
                for o in self.ops[e]:
                    waits = {}
                    for d in o.deps:
                        key = d.sem.num
                        if waits.get(key, (None, 0))[1] < d.val:
                            waits[key] = (d.sem, d.val)
                    if o.guard is not None:
                        key = o.guard[0].num
                        if waits.get(key, (None, 0))[1] < o.guard[1]:
                            waits[key] = o.guard
                    for key, (sem, val) in waits.items():
                        if seen.get(key, 0) >= val:
                            continue
                        seen[key] = val
                        eng.wait_ge(sem, val)
                    ins = o.fn(eng)
                    if o.dma:
                        ins.then_inc(o.sem, 16 if o.dma is True else o.dma)
                    elif o.sig:
                        ins.then_inc(o.sem, 1)
                if e == 'sp':
                    for sem, v in final:
                        if seen.get(sem.num, 0) < v:
                            eng.wait_ge(sem, v)
            return body

        with nc.Block() as block:
            block.tensor(run('pe'))
            block.scalar(run('act'))
            block.vector(run('dve'))
            block.gpsimd(run('pool'))
            block.sync(run('sp'))


def build():
    nc = bass.Bass("TRN2", target_bir_lowering=False)
    s = Sched(nc)

    def din(name, shape, dt=F32):
        return nc.dram_tensor(name, shape, dt, kind="ExternalInput").ap()

    def dout(name, shape):
        return nc.dram_tensor(name, shape, F32, kind="ExternalOutput").ap()

    xT_d = din("xT", [D, 1024])
    cT_d = din("cT", [128, 16])
    pp_d = din("pp", [128, NPP])
    cs_d = din("cs", [128, 1024])
    mats_d = din("mats", [128, 384])
    masks_d = din("masks", [128, 18])
    ckd_d = din("ckd", [L, 512, 256])
    ckg_d = din("ckg", [L, 2, 128, 256])
    cvd_d = din("cvd", [L, 256, 512])
    cvg_d = din("cvg", [L, 256, 256])
    w_ada = din("w_ada", [L, D, 6144])
    w_in = din("w_in", [L, D, 6400])
    w_diff_o = din("w_diff_o", [L, 512, D])
    w_gqa_o = din("w_gqa_o", [L, 512, D])
    w_conv_o = din("w_conv_o", [L, 512, D])
    w_o = din("w_o", [L, D, D])
    w_mlp1 = din("w_mlp1", [L, D, 4096])
    w_mlp2 = din("w_mlp2", [L, 4096, D])
    yT_d = dout("yT", [D, 1024])
    okd_d = dout("okd", [L, 512, 512])
    okg_d = dout("okg", [L, 128, 512])
    ovd_d = dout("ovd", [L, 512, 512])
    ovg_d = dout("ovg", [L, 512, 128])
    xsT_d = din("xsT", [4, D, 512])
    cs4_d = din("cs4", [4, 128, 1024])
    kv_d = [nc.dram_tensor("kv%d" % l, [4 * 128, FROW], BF16).ap() for l in range(L)]
    xs1_d = nc.dram_tensor("xs1", [4, D, 512], F32).ap()
    NWD = 112
    wcd_d = nc.dram_tensor("wcd", [NWD, 128, 2048], BF16).ap()

    cnt = [0]

    def sb(shape, dt, name=None):
        cnt[0] += 1
        return nc.alloc_sbuf_tensor("sb_" + (name or ("t%d" % cnt[0])), shape, dt)

    x = sb([128, 8, 1024], F32, "x")
    xb = [[Buf() for _ in range(2)] for _ in range(8)]
    h = sb([128, 8, 1024], BF16, "h")
    hb = [[Buf() for _ in range(2)] for _ in range(8)]
    pp = sb([128, NPP], F32, "pp")
    ppb = Buf()
    cs = sb([128, 1024], F32, "cs")
    csb = Buf()
    matsf = sb([128, 384], F32, "matsf")
    matsb = sb([128, 384], BF16, "matsb")
    matb = Buf()
    masks = sb([128, 18], F32, "masks")
    maskb = Buf()
    cT = sb([128, 16], F32, "cT")
    scT = sb([128, 16], BF16, "scT")
    ctb = Buf()
    mod = sb([128, L, 48, 2], F32, "mod")
    modb = Buf()
    drv = sb([128, L, 2, 6, 8], F32, "drv")
    drvb = Buf()
    lam = sb([128, L, 4], F32, "lam")
    lamb = Buf()
    epsb = sb([128, 1], F32, "epsb")
    epsbb = Buf()
    ones_f = matsf[:, 256:384]
    rot_b = matsb[:, 0:128]
    bd_f = matsf[:, 128:256]
    ones_b = matsb[:, 256:384]

    NST, NWB = 2, 3
    wst = [sb([128, 2048], F32, "wst%d" % i) for i in range(NST)]
    wstb = [Buf() for _ in range(NST)]
    wbf = [sb([128, 2048], BF16, "wbf%d" % i) for i in range(NWB)]
    wbfb = [Buf() for _ in range(NWB)]
    wk = [0, 0]

    NT = 6
    tf = [sb([128, 512], F32, "tf%d" % i) for i in range(NT)]
    tfb = [Buf() for _ in range(NT)]
    LT = [sb([128, 512], F32, "lt%d" % i) for i in range(3)]
    LTb = [Buf() for _ in range(3)]
    tk = [0]
    NE = 4
    te = [sb([128, 512], BF16, "te%d" % i) for i in range(NE)]
    teb = [Buf() for _ in range(NE)]
    ek = [0]

    def tmpf():
        i = tk[0] % NT
        tk[0] += 1
        return tf[i], tfb[i]

    def tmpe():
        i = ek[0] % NE
        ek[0] += 1
        return te[i], teb[i]

    pst = [nc.alloc_psum_tensor("ps%d" % i, [128, 512], F32) for i in range(8)]
    psb = [Buf(excl=True) for _ in range(8)]
    pk = [0]

    def psum():
        i = pk[0] % 4
        pk[0] += 1
        return pst[i], psb[i]

    ak = [0]

    def psum_acc():
        i = 4 + (ak[0] % 4)
        ak[0] += 1
        return pst[i], psb[i]

    RB = 92 * 1024
    reg = sb([128, RB // 2], BF16, "region")

    def rview(off, nbytes, dt, pattern=None, **kw):
        v = reg[:, off // 2:(off + nbytes) // 2]
        if dt == F32:
            v = v.bitcast(F32)
        if pattern:
            v = v.rearrange(pattern, **kw)
        return v

    o = 0
    dqT = rview(o, 8192, BF16, "p (c n) -> p c n", c=4); o += 8192
    gqT = rview(o, 8192, BF16, "p (c n) -> p c n", c=4); o += 8192
    dkT = rview(o, 8192, BF16, "p (c n) -> p c n", c=4); o += 8192
    gkT = rview(o, 4096, BF16, "p (c n) -> p c n", c=2); o += 4096
    Vd = rview(o, 8192, BF16, "p (t n) -> p t n", t=8); o += 8192
    Vg = rview(o, 4096, BF16, "p (t n) -> p t n", t=8); o += 4096
    UW = 1114
    upad = rview(o, 4 * UW * 4, F32, "p (c n) -> p c n", c=4); o += 4 * UW * 4
    o = (o + 63) // 64 * 64
    edge = rview(o, 512, BF16); o += 512
    xtmp_off = o
    xtmp = rview(o, 8192, BF16, "p (j n) -> p j n", j=8); o += 13312
    gK = rview(xtmp_off, 2304 * 2, BF16)
    gV = rview(xtmp_off + 4608, 18 * 128 * 2, BF16, "p (t n) -> p t n", t=18)
    gA = rview(xtmp_off + 9216, 4096, BF16, "p (a n) -> p a n", a=4)
    ucT = rview(xtmp_off, 8192, BF16, "p (c n) -> p c n", c=4)
    off_da = o
    daT = rview(o, 8192, BF16, "p (c n) -> p c n", c=4); o += 8192
    gaT = rview(o, 8192, BF16, "p (c n) -> p c n", c=4); o += 8192
    sgv = rview(off_da, 16384, F32, "p (c n) -> p c n", c=4)
    cacc = rview(o, 4352, F32); o += 4352
    assert o <= RB, o
    DBG.update(dict(off_da=off_da, xtmp_off=xtmp_off, upad_off=40960, UW=UW))
    merged = rview(0, 16384, BF16, "p (c n) -> p c n", c=8)
    hid = rview(16384, 65536, BF16, "p (c n) -> p c n", c=32)
    hedge = sb([128, 8, 30], BF16, "hedge")
    hedgeb = Buf()
    BL = [0, 1]
    KVB = [0]
    WC = {'on': False, 'map': {}, 'n': 0}
    WD = {'mode': 'off', 'map': {}}
    wcd_buf = [Buf() for _ in range(112)]
    wc_offs = [40960, 45056, 49152, 53248, 59328, 63424, 67520, 72640, 76736, 80832, 84928]
    wc_view = [rview(o_, 4096, BF16) for o_ in wc_offs]
    wc_buf = [Buf() for _ in wc_offs]

    B = {k: Buf(k) for k in ['dqT0', 'dqT1', 'gqT0', 'gqT1', 'dkT0', 'dkT1', 'gkT0', 'gkT1', 'Vd0', 'Vd1', 'Vg0', 'Vg1',
                             'upad', 'edge', 'xtmp', 'gK', 'gV', 'gA', 'daT', 'gaT', 'ucT', 'cacc', 'halo',
                             'merged', 'hid0', 'hid1', 'xs1_0', 'xs1_1', 'xs1_2', 'xs1_3', 'out']}
    for l_ in range(L):
        for i_ in range(4):
            B['kv%d_%d' % (l_, i_)] = Buf()

    def V(op, *a, **k):
        return op

    def dma(out, in_, reads=(), writes=()):
        s.op('sp', lambda e: e.dma_start(out=out, in_=in_), reads=reads, writes=writes, dma=True)

    def load_w(src):
        rows, n = src.shape
        kc = rows // 128
        assert kc * n <= 2048, (kc, n)
        key = repr(src)
        if WC['on'] and key in WC['map']:
            return WC['map'][key]
        if WD['mode'] == 'read' and key in WD['map']:
            di = WD['map'][key]
            j = wk[1] % NWB
            wk[1] += 1
            dma(wbf[j][:, 0:kc * n], wcd_d[di][:, 0:kc * n], reads=[wcd_buf[di]], writes=[wbfb[j]])
            return wbf[j][:, 0:kc * n].rearrange("p (k n) -> p k n", k=kc), wbfb[j]
        i = wk[0] % NST
        wk[0] += 1
        stv = wst[i][:, 0:kc * n].rearrange("p (k n) -> p k n", k=kc)
        dma(stv, src.rearrange("(k p) n -> p k n", p=128), writes=[wstb[i]])
        if WC['on']:
            ci = WC['n']
            WC['n'] += 1
            bfv = wc_view[ci][:, 0:kc * n].rearrange("p (k n) -> p k n", k=kc)
            wk[1] += 1
            if wk[1] % 2 == 0:
                s.op('pool', lambda e: e.tensor_copy(out=bfv, in_=stv), reads=[wstb[i]], writes=[wc_buf[ci]])
            else:
                s.op('act', lambda e: e.activation(out=bfv, in_=stv, func=AF.Copy), reads=[wstb[i]], writes=[wc_buf[ci]])
            WC['map'][key] = (bfv, wc_buf[ci])
            return bfv, wc_buf[ci]
        j = wk[1] % NWB
        wk[1] += 1
        bfv = wbf[j][:, 0:kc * n].rearrange("p (k n) -> p k n", k=kc)
        if wk[1] % 2 == 0:
            s.op('pool', lambda e: e.tensor_copy(out=bfv, in_=stv), reads=[wstb[i]], writes=[wbfb[j]])
        else:
            s.op('act', lambda e: e.activation(out=bfv, in_=stv, func=AF.Copy), reads=[wstb[i]], writes=[wbfb[j]])
        if WD['mode'] == 'write' and key not in WD['map'] and len(WD['map']) < 112:
            di = len(WD['map'])
            WD['map'][key] = di
            dma(wcd_d[di][:, 0:kc * n], wbf[j][:, 0:kc * n], reads=[wbfb[j]], writes=[wcd_buf[di]])
        return bfv, wbfb[j]

    def mm(out, lhsT, rhs, start, stop, reads, writes):
        s.op('pe', lambda e: e.matmul(out, lhsT=lhsT, rhs=rhs, start=start, stop=stop), reads=reads, writes=writes)

    def P(l, off, n=1):
        return pp[:, l * PL + off:l * PL + off + n]

    O_N1, O_N2, O_BADA, O_QN, O_KN, O_SUB, O_CW, O_CB, O_LG, O_LB, O_LQ = 0, 8, 16, 64, 65, 66, 67, 191, 195, 199, 203

    dma(pp[:, :], pp_d, writes=[ppb])
    dma(cs[:, :], cs_d, writes=[csb])
    dma(matsf[:, :], mats_d, writes=[matb])
    dma(masks[:, :], masks_d, writes=[maskb])
    dma(cT[:, :], cT_d, writes=[ctb])
    for c in range(8):
        for b in range(2):
            dma(x[:, c, b * 512:(b + 1) * 512], xT_d[c * 128:(c + 1) * 128, b * 512:(b + 1) * 512], writes=[xb[c][b]])
    s.op('dve', lambda e: e.tensor_copy(out=matsb[:, :], in_=matsf[:, :]), reads=[matb], writes=[matb])
    s.op('dve', lambda e: e.memset(epsb[:, :], EPS), writes=[epsbb])
    s.op('act', lambda e: e.activation(out=scT[:, :], in_=cT[:, :], func=AF.Silu), reads=[ctb], writes=[ctb])

    try:
        BG = []

        def ada_tile(l, g4):
            md, wc_on = WD['mode'], WC['on']
            WD['mode'] = 'off'
            WC['on'] = False
            wv, wb_ = load_w(w_ada[l][:, g4 * 256:(g4 + 1) * 256])
            WD['mode'] = md
            WC['on'] = wc_on
            ps, pb = psum()
            for jj in range(2):
                for kc in range(8):
                    mm(ps[:, 2 * jj:2 * jj + 2], wv[:, kc, jj * 128:(jj + 1) * 128], scT[:, 2 * kc:2 * kc + 2],
                       kc == 0, kc == 7, [wb_, ctb], [pb])
            for g in range(2):
                s.op('dve', lambda e, g=g: e.tensor_tensor(
                    out=mod[:, l, 2 * g4:2 * g4 + 2, g], in0=ps[:, 0:4].rearrange("p (j g) -> p j g", g=2)[:, :, g],
                    in1=P(l, O_BADA + 2 * g4, 2), op=ALU.add), reads=[pb, ppb], writes=[modb])

        def bg_drain(n=1):
            for _ in range(min(n, len(BG))):
                BG.pop(0)()

        def ada_finish(l):
            for g in range(2):
                for (k_, on, osc, osh, og) in ((0, O_N1, 8, 0, 16), (3, O_N2, 32, 24, 40)):
                    s.op('dve', lambda e, l=l, g=g, k_=k_, on=on, osc=osc: e.scalar_tensor_tensor(
                        out=drv[:, l, g, k_, :], in0=mod[:, l, osc:osc + 8, g], scalar=1.0, in1=P(l, on, 8),
                        op0=ALU.add, op1=ALU.mult), reads=[modb, ppb], writes=[drvb])
                    s.op('dve', lambda e, l=l, g=g, k_=k_, osh=osh: e.tensor_copy(
                        out=drv[:, l, g, k_ + 1, :], in_=mod[:, l, osh:osh + 8, g]), reads=[modb], writes=[drvb])
                    s.op('dve', lambda e, l=l, g=g, k_=k_, og=og: e.tensor_copy(
                        out=drv[:, l, g, k_ + 2, :], in_=mod[:, l, og:og + 8, g]), reads=[modb], writes=[drvb])
            tv, tb_ = tmpf()
            for i in range(2):
                s.op('dve', lambda e, l=l, i=i, tv=tv: e.tensor_tensor(
                    out=tv[:, i * 64:(i + 1) * 64], in0=P(l, O_LQ + i * 128, 64), in1=P(l, O_LQ + i * 128 + 64, 64),
                    op=ALU.mult), reads=[ppb], writes=[tb_])
                s.op('dve', lambda e, l=l, i=i, tv=tv: e.reduce_sum(out=lam[:, l, 2 + i:3 + i], in_=tv[:, i * 64:(i + 1) * 64],
                                                                  axis=AX.X), reads=[tb_], writes=[lamb])
            s.op('act', lambda e, l=l: e.activation(out=lam[:, l, 2:4], in_=lam[:, l, 2:4], func=AF.Exp), reads=[lamb], writes=[lamb])
            lam_init = 0.8 - 0.6 * math.exp(-0.3 * l)
            s.op('dve', lambda e, l=l, li=lam_init: e.scalar_tensor_tensor(
                out=lam[:, l, 0:1], in0=lam[:, l, 2:3], scalar=li, in1=lam[:, l, 3:4], op0=ALU.add, op1=ALU.subtract),
                reads=[lamb], writes=[lamb])
            s.op('dve', lambda e, l=l: e.tensor_scalar(out=lam[:, l, 1:2], in0=lam[:, l, 0:1], scalar1=-1.0, scalar2=None,
                                                       op0=ALU.mult), reads=[lamb], writes=[lamb])


        for g4_ in range(24):
            ada_tile(0, g4_)
        ada_finish(0)
        for g4_ in range(24):
            BG.append(lambda g4_=g4_: ada_tile(1, g4_))

        def rstd_op(dst, src, scale, reads, dbuf):
            s.op('act', lambda e: e.activation(out=dst, in_=src, func=AF.Sqrt, bias=epsb[:, 0:1], scale=scale), reads=list(reads) + [epsbb], writes=[dbuf])
            s.op('dve', lambda e: e.reciprocal(out=dst, in_=dst), reads=[dbuf], writes=[dbuf])

        def rms_mod(l, ka):
            for b in list(BL):
                ps, pb = psum()
                for c in range(8):
                    tv, tb_ = tmpe()
                    s.op('act', lambda e, c=c, b=b, tv=tv: e.activation(out=tv[:, :], in_=x[:, c, b * 512:(b + 1) * 512],
                                                                      func=AF.Square), reads=[xb[c][b]], writes=[tb_])
                    mm(ps[:, :], ones_b, tv[:, :], c == 0, c == 7, [tb_, matb], [pb])
                rv, rb_ = LT[0], LTb[0]
                rstd_op(rv[:, :], ps[:, :], 1.0 / D, [pb], rb_)
                for c in range(8):
                    tv, tb_ = tmpf()
                    s.op('dve', lambda e, c=c, b=b, tv=tv, rv=rv: e.scalar_tensor_tensor(
                        out=tv[:, :], in0=x[:, c, b * 512:(b + 1) * 512], scalar=drv[:, l, b, ka, c:c + 1], in1=rv[:, :],
                        op0=ALU.mult, op1=ALU.mult), reads=[xb[c][b], drvb, rb_], writes=[tb_])
                    s.op('act', lambda e, c=c, b=b, tv=tv: e.activation(
                        out=h[:, c, b * 512:(b + 1) * 512], in_=tv[:, :], func=AF.Identity,
                        bias=drv[:, l, b, ka + 1, c:c + 1], scale=1.0), reads=[tb_, drvb], writes=[hb[c][b]])

        def hread(b):
            return [hb[c][b] for c in range(8)]

        def lin_fm(wsrc_fn, ncols, kcs, rhs_fn, rhs_reads_fn, consume, group=None, blocks=None):
            gcols = group or max(128, min(512, (2048 // kcs) // 128 * 128))
            for c0 in range(0, ncols, gcols):
                n = min(gcols, ncols - c0)
                bg_drain(1)
                wv, wb_ = load_w(wsrc_fn(c0, n))
                for jj in range(n // 128):
                    j = c0 // 128 + jj
                    for b in list(BL if blocks is None else blocks):
                        ps, pb = psum()
                        for kc in range(kcs):
                            mm(ps[:, :], wv[:, kc, jj * 128:(jj + 1) * 128], rhs_fn(kc, b), kc == 0, kc == kcs - 1,
                               [wb_] + rhs_reads_fn(b), [pb])
                        consume(j, b, ps, pb)

        def rope_store(ps, pb, dst, dstb, extra_reads=()):
            t1, t1b = tmpe()
            s.op('act', lambda e: e.activation(out=t1[:, :], in_=ps, func=AF.Copy), reads=[pb] + list(extra_reads), writes=[t1b])
            if ROPEL == 0:
                s.op('dve', lambda e: e.tensor_copy(out=dst, in_=t1[:, :]), reads=[t1b], writes=[dstb])
                return
            p2, p2b = psum()
            mm(p2[:, :], rot_b, t1[:, :], True, True, [t1b, matb], [p2b])
            if ROPEL == 1:
                s.op('dve', lambda e: e.tensor_copy(out=dst, in_=p2[:, :]), reads=[p2b], writes=[dstb])
                return
            t2, t2b = tmpf()
            if ROPEL == 3:
                s.op('dve', lambda e: e.tensor_copy(out=t2[:, :], in_=ps), reads=[pb, csb] + list(extra_reads), writes=[t2b])
            elif ROPEL == 4:
                s.op('dve', lambda e: e.tensor_tensor(out=t2[:, :], in0=cs[:, 512:1024], in1=cs[:, 0:512], op=ALU.mult),
                     reads=[pb, csb] + list(extra_reads), writes=[t2b])
            else:
                s.op('dve', lambda e: e.tensor_tensor(out=t2[:, :], in0=ps, in1=cs[:, 0:512], op=ALU.mult),
                     reads=[pb, csb] + list(extra_reads), writes=[t2b])
            if ROPEL in (2, 3, 4):
                s.op('dve', lambda e: e.tensor_copy(out=dst, in_=t2[:, :]), reads=[t2b, p2b], writes=[dstb])
                return
            t3, t3b = tmpf()
            s.op('dve', lambda e: e.tensor_tensor(out=t3[:, :], in0=p2[:, :], in1=cs[:, 512:1024], op=ALU.mult),
                 reads=[p2b, csb], writes=[t3b])
            s.op('dve', lambda e: e.tensor_tensor(out=dst, in0=t2[:, :], in1=t3[:, :], op=ALU.add),
                 reads=[t2b, t3b], writes=[dstb])

        def headnorm(ps, pb, gain_ap, l):
            sq, sqb = tmpe()
            s.op('act', lambda e: e.activation(out=sq[:, :], in_=ps, func=AF.Square), reads=[pb], writes=[sqb])
            p2, p2b = psum()
            mm(p2[:, :], matsb[:, 128:256], sq[:, :], True, True, [sqb, matb], [p2b])
            rv, rb_ = tmpf()
            rstd_op(rv[:, :], p2[:, :], 1.0 / 64, [p2b], rb_)
            ov, ob = tmpf()
            s.op('dve', lambda e: e.scalar_tensor_tensor(out=ov[:, :], in0=ps, scalar=gain_ap, in1=rv[:, :], op0=ALU.mult,
                                                         op1=ALU.mult), reads=[pb, rb_, ppb], writes=[ob])
            return ov, ob

        def attend(nq, q_ap, q_reads, ktiles, e_dim, out_fn):
            po, pob = psum_acc()
            pd, pdb = psum_acc()
            nk = len(ktiles)
            pend = None

            def pv(i, v_ap, rd, ev, eb):
                mm(po[0:e_dim, 0:nq], v_ap, ev[:, 0:nq], i == 0, i == nk - 1, [eb] + list(rd), [pob])
                mm(pd[:, 0:nq], ones_b, ev[:, 0:nq], i == 0, i == nk - 1, [eb, matb], [pdb])

            for i, (kT, v_ap, rd) in enumerate(ktiles):
                pss, psb_ = psum()
                mm(pss[:, 0:nq], kT, q_ap, True, True, list(rd) + list(q_reads), [psb_])
                ev, eb = tmpe()
                s.op('act', lambda e, pss=pss, ev=ev: e.activation(out=ev[:, 0:nq], in_=pss[:, 0:nq], func=AF.Exp, scale=0.125),
                     reads=[psb_], writes=[eb])
                if pend is not None:
                    pv(*pend)
                pend = (i, v_ap, rd, ev, eb)
            pv(*pend)
            rv, rb_ = tmpf()
            s.op('dve', lambda e: e.reciprocal(out=rv[:, 0:nq], in_=pd[:, 0:nq]), reads=[pdb], writes=[rb_])
            out_fn(po, pob, rv, rb_)

        chk('A')

        def kvproj(l):
            if not KVB:
                return
            hr = lambda kc, b: h[:, kc, b * 512:(b + 1) * 512]
            pass
            def c_dk(j, b, ps, pb):
                if b == 0:
                    s.op('act', lambda e: e.activation(out=dkT[:, j, 0:512], in_=ps[:, :], func=AF.Copy), reads=[pb], writes=[B['dkT0']])
                    tv, tb_ = tmpf()
                    s.op('dve', lambda e: e.tensor_copy(out=tv[:, :], in_=ps[:, :]), reads=[pb], writes=[tb_])
                    dma(okd_d[l][j * 128:(j + 1) * 128, :], tv[:, :], reads=[tb_])
                else:
                    rope_store(ps[:, :], pb, dkT[:, j, 512:1024], B['dkT1'])
            lin_fm(lambda c0, n: w_in[l][:, 512 + c0:512 + c0 + n], 512, 8, hr, hread, c_dk, group=256, blocks=KVB)

            pass
            def load_dup(col0):
                key = ('dup', l, col0)
                if WC['on'] and key in WC['map']:
                    return WC['map'][key]
                i = wk[0] % NST; wk[0] += 1
                j = wk[1] % NWB; wk[1] += 1
                stv = wst[i][:, 0:2048].rearrange("p (k n) -> p k n", k=8)
                for n in range(2):
                    for d_ in range(2):
                        dma(stv[:, :, (2 * n + d_) * 64:(2 * n + d_ + 1) * 64],
                            w_in[l][:, col0 + n * 64:col0 + (n + 1) * 64].rearrange("(k p) n -> p k n", p=128), writes=[wstb[i]])
                if WC['on']:
                    ci = WC['n']
                    WC['n'] += 1
                    bfv = wc_view[ci][:, 0:2048].rearrange("p (k n) -> p k n", k=8)
                    s.op('pool', lambda e: e.tensor_copy(out=bfv, in_=stv), reads=[wstb[i]], writes=[wc_buf[ci]])
                    WC['map'][key] = (bfv, wc_buf[ci])
                    return bfv, wc_buf[ci]
                bfv = wbf[j][:, 0:2048].rearrange("p (k n) -> p k n", k=8)
                s.op('pool', lambda e: e.tensor_copy(out=bfv, in_=stv), reads=[wstb[i]], writes=[wbfb[j]])
                return bfv, wbfb[j]

            wv, wb_ = load_dup(2048)
            for n in range(2):
                for b in list(KVB):
                    ps, pb = psum()
                    for kc in range(8):
                        mm(ps[:, :], wv[:, kc, n * 128:(n + 1) * 128], hr(kc, b), kc == 0, kc == 7, [wb_] + hread(b), [pb])
                    ov, ob = headnorm(ps[:, :], pb, P(l, O_KN), l)
                    if b == 0:
                        s.op('act', lambda e, n=n, ov=ov: e.activation(out=gkT[:, n, 0:512], in_=ov[:, :], func=AF.Copy),
                             reads=[ob], writes=[B['gkT0']])
                        dma(okg_d[l][n * 64:(n + 1) * 64, :], ov[0:64, :], reads=[ob])
                    else:
                        rope_store(ov[:, :], ob, gkT[:, n, 512:1024], B['gkT1'])

            pass
            for half in range(2):
                wv, wb_ = load_w(w_in[l][:, 1024 + half * 256:1024 + (half + 1) * 256])
                for t in range(8):
                    b = t // 4
                    if b not in KVB:
                        continue
                    ps, pb = psum()
                    for kc in range(8):
                        mm(ps[:, 0:256], h[:, kc, t * 128:(t + 1) * 128], wv[:, kc, :], kc == 0, kc == 7, [wb_] + hread(b), [pb])
                    s.op('act', lambda e, t=t, half=half, ps=ps: e.activation(out=Vd[:, t, half * 256:(half + 1) * 256], in_=ps[:, 0:256],
                                                                            func=AF.Copy), reads=[pb], writes=[B['Vd%d' % b]])
                    if b == 0:
                        tv, tb_ = tmpf()
                        s.op('dve', lambda e, tv=tv, ps=ps: e.tensor_copy(out=tv[:, 0:256], in_=ps[:, 0:256]), reads=[pb], writes=[tb_])
                        dma(ovd_d[l][t * 128:(t + 1) * 128, half * 256:(half + 1) * 256], tv[:, 0:256], reads=[tb_])
            pass
            wv, wb_ = load_dup(2176)
            for t in range(8):
                b = t // 4
                if b not in KVB:
                    continue
                ps, pb = psum()
                for kc in range(8):
                    mm(ps[:, 0:256], h[:, kc, t * 128:(t + 1) * 128], wv[:, kc, :], kc == 0, kc == 7, [wb_] + hread(b), [pb])
                s.op('act', lambda e, t=t, ps=ps: e.activation(out=Vg[:, t, :], in_=ps[:, 0:256], func=AF.Copy), reads=[pb],
                     writes=[B['Vg%d' % b]])
                if b == 0:
                    tv, tb_ = tmpf()
                    s.op('dve', lambda e, tv=tv, ps=ps: e.tensor_copy(
                        out=tv[:, 0:128].rearrange("p (n d) -> p n d", n=2),
                        in_=ps[:, 0:256].rearrange("p (n d) -> p n d", n=2)[:, :, 0:64]), reads=[pb], writes=[tb_])
                    dma(ovg_d[l][t * 128:(t + 1) * 128, :], tv[:, 0:128], reads=[tb_])


        def load_x(src, rd):
            for c in range(8):
                dma(x[:, c, 512:1024], src[c * 128:(c + 1) * 128, :], reads=rd, writes=[xb[c][1]])

        def prepass(l, slot):
            BL[:] = [1]
            KVB[:] = [1]
            if l == 0:
                load_x(xsT_d[slot], [])
            else:
                load_x(xs1_d[slot], [B['xs1_%d' % slot]])
            dma(cs[:, :], cs4_d[slot], writes=[csb])
            rms_mod(l, 0)
            kvproj(l)
            s.op('dve', lambda e: e.tensor_copy(out=hedge[:, :, 0:15], in_=h[:, :, 512:527]), reads=hread(1), writes=[hedgeb])
            s.op('dve', lambda e: e.tensor_copy(out=hedge[:, :, 15:30], in_=h[:, :, 1009:1024]), reads=hread(1), writes=[hedgeb])
            s.op('dve', lambda e: e.memset(edge[:, 0:128], 0.0), writes=[B['edge']])
            edge3 = edge[:, 0:120].rearrange("p (c n) -> p c n", c=4)
            sge, sgeb = LT[1], LTb[1]
            sge3 = sge[:, 0:120].rearrange("p (c n) -> p c n", c=4)
            for part, col0 in ((0, 2816), (1, 2304)):
                for c0 in (0, 256):
                    wv, wb_ = load_w(w_in[l][:, col0 + c0:col0 + c0 + 256])
                    for jj in range(2):
                        j = c0 // 128 + jj
                        ps, pb = psum()
                        for kc in range(8):
                            mm(ps[:, 0:30], wv[:, kc, jj * 128:(jj + 1) * 128], hedge[:, kc, :], kc == 0, kc == 7, [wb_, hedgeb], [pb])
                        if part == 0:
                            s.op('act', lambda e, j=j, ps=ps: e.activation(out=sge3[:, j, :], in_=ps[:, 0:30], func=AF.Sigmoid),
                                 reads=[pb], writes=[sgeb])
                        else:
                            s.op('dve', lambda e, j=j, ps=ps: e.tensor_tensor(out=edge3[:, j, :], in0=ps[:, 0:30], in1=sge3[:, j, :], op=ALU.mult),
                                 reads=[pb, sgeb], writes=[B['edge']])
            kb = B['kv%d_%d' % (l, slot)]
            rows = kv_d[l][slot * 128:(slot + 1) * 128, :]
            for c in range(4):
                dma(rows[:, c * 512:(c + 1) * 512], dkT[:, c, 512:1024], reads=[B['dkT1']], writes=[kb])
            for n in range(2):
                dma(rows[:, (4 + n) * 512:(5 + n) * 512], gkT[:, n, 512:1024], reads=[B['gkT1']], writes=[kb])
            for t in range(4):
                dma(rows[:, (6 + t) * 512:(7 + t) * 512], Vd[:, 4 + t, :], reads=[B['Vd1']], writes=[kb])
            for k in range(2):
                dma(rows[:, (10 + k) * 512:(11 + k) * 512].rearrange("p (a n) -> p a n", a=2), Vg[:, 4 + 2 * k:6 + 2 * k, :],
                    reads=[B['Vg1']], writes=[kb])
            dma(rows[:, 12 * 512:12 * 512 + 128], edge[:, 0:128], reads=[B['edge']], writes=[kb])

        def layer(l, slot, with_prompt):
            BL[:] = [0, 1] if with_prompt else [1]
            KVB[:] = [0] if with_prompt else []
            qs = lambda b: [B['dqT%d' % b], B['gqT%d' % b]]
            rms_mod(l, 0)
            hr = lambda kc, b: h[:, kc, b * 512:(b + 1) * 512]
            chk('A1_%d' % l)

            def c_dq(j, b, ps, pb):
                if b == 0 or NOROPE:
                    b0 = b * 512
                    s.op('act', lambda e: e.activation(out=dqT[:, j, b0:b0 + 512], in_=ps[:, :], func=AF.Copy), reads=[pb], writes=[B['dqT%d' % b]])
                elif b == 0:
                    s.op('act', lambda e: e.activation(out=dqT[:, j, 0:512], in_=ps[:, :], func=AF.Copy), reads=[pb], writes=[B['dqT0']])
                else:
                    rope_store(ps[:, :], pb, dqT[:, j, 512:1024], B['dqT1'])
            lin_fm(lambda c0, n: w_in[l][:, c0:c0 + n], 512, 8, hr, hread, c_dq, group=256)

            chk('A3')
            def c_gq(j, b, ps, pb):
                ov, ob = headnorm(ps[:, :], pb, P(l, O_QN), l)
                if b == 0:
                    s.op('act', lambda e: e.activation(out=gqT[:, j, 0:512], in_=ov[:, :], func=AF.Copy), reads=[ob], writes=[B['gqT0']])
                else:
                    rope_store(ov[:, :], ob, gqT[:, j, 512:1024], B['gqT1'])
            lin_fm(lambda c0, n: w_in[l][:, 1536 + c0:1536 + c0 + n], 512, 8, hr, hread, c_gq, group=256)

            kvproj(l)
            chk('A7')
            s.op('pool', lambda e: e.memset(upad[:, :, :], 0.0), writes=[B['upad']])
            SGB = Buf()
            Sched.alias([SGB], [B['daT'], B['gaT']])

            def c_cg(j, b, ps, pb):
                s.op('act', lambda e: e.activation(out=sgv[:, j, b * 512:(b + 1) * 512], in_=ps[:, :], func=AF.Sigmoid), reads=[pb], writes=[SGB])
            lin_fm(lambda c0, n: w_in[l][:, 2816 + c0:2816 + c0 + n], 512, 8, hr, hread, c_cg, group=256)
            useg = [(0, 15), (256, 301), (512, 587)]

            def c_ca(j, b, ps, pb):
                if b == 0:
                    for sq_ in range(2):
                        t0, u0 = useg[sq_]
                        s.op('dve', lambda e, t0=t0, u0=u0: e.tensor_tensor(out=upad[:, j, u0:u0 + 256], in0=ps[:, t0:t0 + 256],
                                                                           in1=sgv[:, j, t0:t0 + 256], op=ALU.mult),
                             reads=[pb, SGB], writes=[B['upad']])
                else:
                    s.op('dve', lambda e: e.tensor_tensor(out=upad[:, j, 587:587 + 512], in0=ps[:, :], in1=sgv[:, j, 512:1024], op=ALU.mult),
                         reads=[pb, SGB], writes=[B['upad']])
            lin_fm(lambda c0, n: w_in[l][:, 2304 + c0:2304 + c0 + n], 512, 8, hr, hread, c_ca, group=256)

            Sched.alias([B['daT'], B['gaT']], [SGB])
            chk('B%d' % l)
            kvv = kv_d[l].rearrange("(j p) f -> p j f", p=128)
            kvr = [B['kv%d_%d' % (l, i_)] for i_ in range(4)]

            def gather(u, n, dst, dstb, c0=0):
                dma(dst, kvv[:, :, u * 512 + c0:u * 512 + c0 + n], reads=kvr, writes=[dstb])

            Sched.alias([B['gK'], B['gV'], B['gA']], [B['xtmp']])
            sl, sr = (slot - 1) % 4, (slot + 1) % 4
            dma(gA[:, 0, 0:128], kv_d[l][sl * 128:(sl + 1) * 128, 12 * 512:12 * 512 + 128], reads=kvr, writes=[B['gA']])
            dma(gA[:, 1, 0:128], kv_d[l][sr * 128:(sr + 1) * 128, 12 * 512:12 * 512 + 128], reads=kvr, writes=[B['gA']])
            hal = gA[:, 0:2, 0:120].rearrange("p j (c n) -> p j c n", c=4)
            s.op('dve', lambda e: e.scalar_tensor_tensor(out=upad[:, :, 572:587], in0=hal[:, 0, :, 15:30], scalar=masks[:, slot:slot + 1],
                                                         in1=upad[:, :, 572:587], op0=ALU.mult, op1=ALU.add),
                 reads=[B['gA'], maskb, B['upad']], writes=[B['upad']])
            s.op('dve', lambda e: e.scalar_tensor_tensor(out=upad[:, :, 1099:1114], in0=hal[:, 1, :, 0:15], scalar=masks[:, 4 + slot:5 + slot],
                                                         in1=upad[:, :, 1099:1114], op0=ALU.mult, op1=ALU.add),
                 reads=[B['gA'], maskb, B['upad']], writes=[B['upad']])
            NV = 1084
            conv_ops = []
            for c in range(4):
                for k in range(31):
                    if k == 0:
                        conv_ops.append(lambda c=c: s.op('dve', lambda e: e.tensor_scalar(out=cacc[:, 0:NV], in0=upad[:, c, 0:NV], scalar1=P(l, O_CW + c * 31),
                                                                                          scalar2=P(l, O_CB + c), op0=ALU.mult, op1=ALU.add),
                                                         reads=[B['upad'], ppb], writes=[B['cacc']]))
                    else:
                        conv_ops.append(lambda c=c, k=k: s.op('dve', lambda e: e.scalar_tensor_tensor(out=cacc[:, 0:NV], in0=upad[:, c, k:k + NV],
                                                                                                      scalar=P(l, O_CW + c * 31 + k), in1=cacc[:, 0:NV],
                                                                                                      op0=ALU.mult, op1=ALU.add),
                                                              reads=[B['upad'], ppb, B['cacc']], writes=[B['cacc']]))
                conv_ops.append(lambda c=c: s.op('dve', lambda e: e.tensor_copy(out=upad[:, c, 15:15 + NV], in_=cacc[:, 0:NV]),
                                                 reads=[B['cacc']], writes=[B['upad']]))
            n_calls = [(32 if with_prompt else 0) + 16]

            def drain(final=False):
                n = len(conv_ops) if final else -(-len(conv_ops) // max(1, n_calls[0]))
                n_calls[0] -= 1
                for _ in range(min(n, len(conv_ops))):
                    conv_ops.pop(0)()

            def diff_out(l, h_, tok0, nq):
                st = {}

                def fn(comp):
                    def out_fn(po, pob, rv, rb_):
                        tv, tb_ = (LT[2], LTb[2]) if comp == 0 else tmpf()
                        s.op('dve', lambda e: e.tensor_tensor(out=tv[:, 0:nq], in0=po[:, 0:nq], in1=rv[:, 0:nq], op=ALU.mult),
                             reads=[pob, rb_], writes=[tb_])
                        st[comp] = (tv, tb_)
                        if comp == 1:
                            t0v, t0b = st[0]
                            dv_, db_ = tmpf()
                            s.op('dve', lambda e: e.scalar_tensor_tensor(out=dv_[:, 0:nq], in0=tv[:, 0:nq], scalar=lam[:, l, 1:2],
                                                                         in1=t0v[:, 0:nq], op0=ALU.mult, op1=ALU.add),
                                 reads=[tb_, t0b, lamb], writes=[db_])
                            sq, sqb = tmpe()
                            s.op('act', lambda e: e.activation(out=sq[:, 0:nq], in_=dv_[:, 0:nq], func=AF.Square), reads=[db_], writes=[sqb])
                            p2, p2b = psum()
                            mm(p2[:, 0:nq], ones_b, sq[:, 0:nq], True, True, [sqb, matb], [p2b])
                            r2, r2b = tmpf()
                            rstd_op(r2[:, 0:nq], p2[:, 0:nq], 1.0 / 128, [p2b], r2b)
                            s.op('dve', lambda e: e.scalar_tensor_tensor(out=dv_[:, 0:nq], in0=dv_[:, 0:nq], scalar=P(l, O_SUB),
                                                                         in1=r2[:, 0:nq], op0=ALU.mult, op1=ALU.mult),
                                 reads=[db_, r2b, ppb], writes=[db_])
                            li = 0.8 - 0.6 * math.exp(-0.3 * l)
                            s.op('act', lambda e: e.activation(out=daT[:, h_, tok0:tok0 + nq], in_=dv_[:, 0:nq], func=AF.Identity,
                                                               scale=1.0 - li), reads=[db_], writes=[B['daT']])
                    return out_fn
                return fn

            def gqa_out(c, par, tok0, nq):
                def out_fn(po, pob, rv, rb_):
                    s.op('dve', lambda e: e.tensor_tensor(out=gaT[par * 64:(par + 1) * 64, c, tok0:tok0 + nq],
                                                          in0=po[par * 64:(par + 1) * 64, 0:nq], in1=rv[par * 64:(par + 1) * 64, 0:nq],
                                                          op=ALU.mult), reads=[pob, rb_], writes=[B['gaT']])
                return out_fn

            for sq_ in (range(2) if with_prompt else ()):
                tok0 = sq_ * 256
                for h_ in range(4):
                    of = diff_out(l, h_, tok0, 256)
                    for comp in range(2):
                        r0 = comp * 64
                        kts = [(dkT[r0:r0 + 64, h_, tok0 + kt * 128:tok0 + (kt + 1) * 128],
                                Vd[:, sq_ * 2 + kt, h_ * 128:(h_ + 1) * 128], [B['dkT0'], B['Vd0']]) for kt in range(2)]
                        attend(256, dqT[r0:r0 + 64, h_, tok0:tok0 + 256], [B['dqT0']], kts, 128, of(comp))
                        drain()
                for n in range(2):
                    for g in range(4):
                        c = n * 2 + g // 2
                        par = g % 2
                        r0 = par * 64
                        kts = [(gkT[r0:r0 + 64, n, tok0 + kt * 128:tok0 + (kt + 1) * 128],
                                Vg[:, sq_ * 2 + kt, n * 128:(n + 1) * 128], [B['gkT0'], B['Vg0']]) for kt in range(2)]
                        attend(256, gqT[r0:r0 + 64, c, tok0:tok0 + 256], [B['gqT0']], kts, 128, gqa_out(c, par, tok0, 256))
                        drain()

            chk('C%d' % l)
            chk('D%d' % l)
            for h_ in range(4):
                gather(h_, 512, gK[:, 0:2048].rearrange("p (j n) -> p j n", j=4), B['gK'])
                tv, tb_ = tmpf()
                dma(tv[:, 0:256], ckd_d[l][h_ * 128:(h_ + 1) * 128, :], writes=[tb_])
                s.op('pool', lambda e, tv=tv: e.tensor_copy(out=gK[:, 2048:2304], in_=tv[:, 0:256]), reads=[tb_], writes=[B['gK']])
                for t in range(4):
                    gather(6 + t, 128, gV[:, 0:16, :].rearrange("p (j t) n -> p j t n", t=4)[:, :, t, :], B['gV'], c0=h_ * 128)
                for t in range(2):
                    tv, tb_ = tmpf()
                    dma(tv[:, 0:128], cvd_d[l][t * 128:(t + 1) * 128, h_ * 128:(h_ + 1) * 128], writes=[tb_])
                    s.op('pool', lambda e, tv=tv, t=t: e.tensor_copy(out=gV[:, 16 + t, :], in_=tv[:, 0:128]), reads=[tb_], writes=[B['gV']])
                of = diff_out(l, h_, 512, 512)
                for comp in range(2):
                    r0 = comp * 64
                    kts = [(gK[r0:r0 + 64, kt * 128:(kt + 1) * 128], gV[:, kt, :], [B['gK'], B['gV']]) for kt in range(18)]
                    attend(512, dqT[r0:r0 + 64, h_, 512:1024], [B['dqT1']], kts, 128, of(comp))
                    drain()
            for n in range(2):
                gather(4 + n, 512, gK[:, 0:2048].rearrange("p (j n) -> p j n", j=4), B['gK'])
                tv, tb_ = tmpf()
                dma(tv[:, 0:256], ckg_d[l][n], writes=[tb_])
                s.op('pool', lambda e, tv=tv: e.tensor_copy(out=gK[:, 2048:2304], in_=tv[:, 0:256]), reads=[tb_], writes=[B['gK']])
                for t in range(4):
                    gather(10 + t // 2, 128, gV[:, 0:16, :].rearrange("p (j t) n -> p j t n", t=4)[:, :, t, :], B['gV'],
                           c0=(t % 2) * 256 + n * 128)
                for t in range(2):
                    tv, tb_ = tmpf()
                    dma(tv[:, 0:128], cvg_d[l][t * 128:(t + 1) * 128, n * 128:(n + 1) * 128], writes=[tb_])
                    s.op('pool', lambda e, tv=tv, t=t: e.tensor_copy(out=gV[:, 16 + t, :], in_=tv[:, 0:128]), reads=[tb_], writes=[B['gV']])
                for g in range(4):
                    c = n * 2 + g // 2
                    par = g % 2
                    r0 = par * 64
                    kts = [(gK[r0:r0 + 64, kt * 128:(kt + 1) * 128], gV[:, kt, :], [B['gK'], B['gV']]) for kt in range(18)]
                    attend(512, gqT[r0:r0 + 64, c, 512:1024], [B['gqT1']], kts, 128, gqa_out(c, par, 512, 512))
                    drain()

            drain(final=True)
            Sched.alias([B['ucT']], [B['gK'], B['gV'], B['gA']])
            for (t0, u0, n) in (((0, 15, 256), (256, 301, 256), (512, 587, 512)) if with_prompt else ((512, 587, 512),)):
                pm, pmb = psum()
                pq, pqb = psum()
                for c in range(4):
                    mm(pm[:, 0:n], ones_f, upad[:, c, u0:u0 + n], c == 0, c == 3, [B['upad'], matb], [pmb])
                for c in range(4):
                    sq, sqb = tmpf()
                    s.op('act', lambda e, c=c, sq=sq, u0=u0, n=n: e.activation(out=sq[:, 0:n], in_=upad[:, c, u0:u0 + n], func=AF.Square),
                         reads=[B['upad']], writes=[sqb])
                    mm(pq[:, 0:n], ones_f, sq[:, 0:n], c == 0, c == 3, [sqb, matb], [pqb])
                mu, mub = LT[0], LTb[0]
                s.op('dve', lambda e, mu=mu, pm=pm, n=n: e.tensor_scalar(out=mu[:, 0:n], in0=pm[:, 0:n], scalar1=1.0 / 512, scalar2=None,
                                                                         op0=ALU.mult), reads=[pmb], writes=[mub])
                var, vb = LT[1], LTb[1]
                s.op('dve', lambda e, var=var, mu=mu, n=n: e.tensor_tensor(out=var[:, 0:n], in0=mu[:, 0:n], in1=mu[:, 0:n], op=ALU.mult),
                     reads=[mub], writes=[vb])
                s.op('dve', lambda e, var=var, pq=pq, n=n: e.scalar_tensor_tensor(out=var[:, 0:n], in0=pq[:, 0:n], scalar=1.0 / 512,
                                                                                  in1=var[:, 0:n], op0=ALU.mult, op1=ALU.subtract),
                     reads=[pqb, vb], writes=[vb])
                rstd_op(var[:, 0:n], var[:, 0:n], 1.0, [vb], vb)
                for c in range(4):
                    tv, tb_ = tmpf()
                    s.op('dve', lambda e, c=c, tv=tv, mu=mu, u0=u0, n=n: e.tensor_tensor(out=tv[:, 0:n], in0=upad[:, c, u0:u0 + n],
                                                                                       in1=mu[:, 0:n], op=ALU.subtract),
                         reads=[B['upad'], mub], writes=[tb_])
                    s.op('dve', lambda e, c=c, tv=tv, var=var, n=n: e.scalar_tensor_tensor(out=tv[:, 0:n], in0=tv[:, 0:n], scalar=P(l, O_LG + c),
                                                                                         in1=var[:, 0:n], op0=ALU.mult, op1=ALU.mult),
                         reads=[tb_, vb, ppb], writes=[tb_])
                    s.op('act', lambda e, c=c, tv=tv, t0=t0, n=n: e.activation(out=ucT[:, c, t0:t0 + n], in_=tv[:, 0:n], func=AF.Silu,
                                                                             bias=P(l, O_LB + c), scale=1.0),
                         reads=[tb_, ppb], writes=[B['ucT']])

            chk('E%d' % l)
            Sched.alias([B['merged']], [B['dqT0'], B['dqT1'], B['gqT0'], B['gqT1']])
            for j in range(8):
                for i, (wsrc, act, ab) in enumerate(((w_diff_o, daT, B['daT']), (w_gqa_o, gaT, B['gaT']), (w_conv_o, ucT, B['ucT']))):
                    wv, wb_ = load_w(wsrc[l][:, j * 128:(j + 1) * 128])
                    gw, gwb = load_w(w_in[l][:, 3328 + i * 1024 + j * 128:3328 + i * 1024 + (j + 1) * 128])
                    for b in list(BL):
                        acc, accb = LT[b], LTb[b]
                        pbr, pbrb = psum()
                        for kc in range(4):
                            mm(pbr[:, :], wv[:, kc, :], act[:, kc, b * 512:(b + 1) * 512], kc == 0, kc == 3, [wb_, ab], [pbrb])
                        pg, pgb = psum()
                        for kc in range(8):
                            mm(pg[:, :], gw[:, kc, :], hr(kc, b), kc == 0, kc == 7, [gwb] + hread(b), [pgb])
                        gv_, gb_ = tmpf()
                        s.op('act', lambda e, gv_=gv_, pg=pg: e.activation(out=gv_[:, :], in_=pg[:, :], func=AF.Sigmoid), reads=[pgb], writes=[gb_])
                        if i == 0:
                            s.op('dve', lambda e, acc=acc, pbr=pbr, gv_=gv_: e.tensor_tensor(out=acc[:, :], in0=pbr[:, :], in1=gv_[:, :], op=ALU.mult),
                                 reads=[pbrb, gb_], writes=[accb])
                        else:
                            s.op('dve', lambda e, pbr=pbr, gv_=gv_: e.tensor_tensor(out=gv_[:, :], in0=pbr[:, :], in1=gv_[:, :], op=ALU.mult),
                                 reads=[pbrb, gb_], writes=[gb_])
                            if i == 1:
                                s.op('dve', lambda e, acc=acc, gv_=gv_: e.tensor_tensor(out=acc[:, :], in0=acc[:, :], in1=gv_[:, :], op=ALU.add),
                                     reads=[accb, gb_], writes=[accb])
                            else:
                                s.op('dve', lambda e, acc=acc, gv_=gv_, j=j, b=b: e.tensor_tensor(out=merged[:, j, b * 512:(b + 1) * 512], in0=acc[:, :],
                                                                                             in1=gv_[:, :], op=ALU.add),
                                     reads=[accb, gb_], writes=[B['merged']])

            def c_wo(j, b, ps, pb):
                s.op('dve', lambda e: e.scalar_tensor_tensor(out=x[:, j, b * 512:(b + 1) * 512], in0=ps[:, :], scalar=drv[:, l, b, 2, j:j + 1],
                                                             in1=x[:, j, b * 512:(b + 1) * 512], op0=ALU.mult, op1=ALU.add),
                     reads=[pb, drvb, xb[j][b]], writes=[xb[j][b]])
            lin_fm(lambda c0, n: w_o[l][:, c0:c0 + n], 1024, 8, lambda kc, b: merged[:, kc, b * 512:(b + 1) * 512], lambda b: [B['merged']], c_wo,
                   group=256)

            chk('F%d' % l)
            rms_mod(l, 3)
            Sched.alias([B['hid0'], B['hid1']], [B[k] for k in ['merged', 'dkT0', 'dkT1', 'gkT0', 'gkT1', 'Vd0', 'Vd1', 'Vg0', 'Vg1', 'upad',
                                                                'edge', 'xtmp', 'gK', 'gV', 'gA', 'daT', 'gaT', 'ucT', 'cacc']])

            def c_m1(j, b, ps, pb):
                tv, tb_ = tmpf()
                s.op('act', lambda e: e.activation(out=tv[:, :], in_=ps[:, :], func=AF.Relu), reads=[pb], writes=[tb_])
                s.op('dve', lambda e: e.tensor_tensor(out=hid[:, j, b * 512:(b + 1) * 512], in0=tv[:, :], in1=tv[:, :], op=ALU.mult),
                     reads=[tb_], writes=[B['hid%d' % b]])
            lin_fm(lambda c0, n: w_mlp1[l][:, c0:c0 + n], 4096, 8, hr, hread, c_m1, group=256)
            for j in range(8):
                wts = [load_w(w_mlp2[l][kh * 2048:(kh + 1) * 2048, j * 128:(j + 1) * 128]) for kh in range(2)]
                for b in list(BL):
                    ps, pb = psum()
                    for kh in range(2):
                        wv, wb_ = wts[kh]
                        for kc in range(16):
                            kk_ = kh * 16 + kc
                            mm(ps[:, :], wv[:, kc, :], hid[:, kk_, b * 512:(b + 1) * 512], kk_ == 0, kk_ == 31, [wb_, B['hid%d' % b]], [pb])
                    s.op('dve', lambda e, j=j, b=b, ps=ps: e.scalar_tensor_tensor(out=x[:, j, b * 512:(b + 1) * 512], in0=ps[:, :],
                                                                                scalar=drv[:, l, b, 5, j:j + 1], in1=x[:, j, b * 512:(b + 1) * 512],
                                                                                op0=ALU.mult, op1=ALU.add),
                         reads=[pb, drvb, xb[j][b]], writes=[xb[j][b]])
            chk('G%d' % l)
            allr = [B[k] for k in ['dqT0', 'dqT1', 'gqT0', 'gqT1', 'dkT0', 'dkT1', 'gkT0', 'gkT1', 'Vd0', 'Vd1', 'Vg0', 'Vg1', 'upad', 'edge',
                                   'xtmp', 'daT', 'gaT', 'cacc']]
            Sched.alias(allr, [B['hid0'], B['hid1']])

        for l_ in range(L):
            if l_ == 1:
                bg_drain(100)
                ada_finish(1)
            shared = [B[k] for k in ['upad', 'xtmp', 'gK', 'gV', 'gA', 'daT', 'gaT', 'ucT', 'cacc', 'hid0', 'hid1']]
            Sched.alias(wc_buf, shared)
            WC['on'], WC['map'], WC['n'] = True, {}, 0
            WD['mode'] = 'off'
            for i_ in ((0, 1, 2, 3) if l_ == 0 else (1, 2, 3, 0)):
                prepass(l_, i_)
            WC['on'] = False
            Sched.alias(shared, wc_buf)
            if l_ == 0:
                for i_ in (3, 2, 1, 0):
                    WD['mode'] = 'write' if i_ == 3 else 'read'
                    load_x(xsT_d[i_], [])
                    dma(cs[:, :], cs4_d[i_], writes=[csb])
                    layer(0, i_, i_ == 3)
                    for c in range(8):
                        dma(xs1_d[i_][c * 128:(c + 1) * 128, :], x[:, c, 512:1024], reads=[xb[c][1]], writes=[B['xs1_%d' % i_]])
            else:
                WD['mode'] = 'off'
                layer(1, 0, True)

        for b in range(2):
            ps, pb = psum()
            for c in range(8):
                tv, tb_ = tmpf()
                s.op('act', lambda e, c=c, b=b, tv=tv: e.activation(out=tv[:, :], in_=x[:, c, b * 512:(b + 1) * 512], func=AF.Square),
                     reads=[xb[c][b]], writes=[tb_])
                mm(ps[:, :], ones_f, tv[:, :], c == 0, c == 7, [tb_, matb], [pb])
            rv, rb_ = LT[0], LTb[0]
            rstd_op(rv[:, :], ps[:, :], 1.0 / D, [pb], rb_)
            for c in range(8):
                tv, tb_ = tmpf()
                s.op('dve', lambda e, c=c, b=b, tv=tv, rv=rv: e.scalar_tensor_tensor(
                    out=tv[:, :], in0=x[:, c, b * 512:(b + 1) * 512], scalar=pp[:, 2 * PL + c:2 * PL + c + 1], in1=rv[:, :],
                    op0=ALU.mult, op1=ALU.mult), reads=[xb[c][b], ppb, rb_], writes=[tb_])
                dma(yT_d[c * 128:(c + 1) * 128, b * 512:(b + 1) * 512], tv[:, :], reads=[tb_])

    except StopBuild:
        pass
    s.emit()
    return nc


def _rope_tables(pos):
    gw = 64
    row = (pos // gw).astype(np.float32)
    col = (pos % gw).astype(np.float32)
    freqs = (10000.0 ** (-np.arange(0, 32, 2, dtype=np.float32) / 32)).astype(np.float32)
    ang_r = row[:, None] * freqs[None, :]
    ang_c = col[:, None] * freqs[None, :]
    ang = np.concatenate([ang_r, ang_r, ang_c, ang_c], -1).astype(np.float32)
    return np.cos(ang).astype(np.float32), np.sin(ang).astype(np.float32)


def kernel(x_prompt, x_sample, cache_diff_k, cache_diff_v, cache_gqa_k, cache_gqa_v, c, c_ctx,
           w_ada, b_ada, norm1, norm2, w_in, diff_lq1, diff_lk1, diff_lq2, diff_lk2, diff_subln,
           w_diff_o, gqa_q_norm, gqa_k_norm, w_gqa_o, conv_dw, conv_dw_b, conv_ln_g, conv_ln_b,
           w_conv_o, w_o, w_mlp1, w_mlp2, final_norm):
    f = lambda a: np.ascontiguousarray(np.asarray(a, dtype=np.float32))
    x_prompt, x_sample = f(x_prompt), f(x_sample)
    cdk, cdv, cgk, cgv = f(cache_diff_k), f(cache_diff_v), f(cache_gqa_k), f(cache_gqa_v)
    c, c_ctx = f(c), f(c_ctx)
    pp = np.zeros((128, NPP), np.float32)
    fm = lambda v, n: f(v).reshape(n, 128).T
    for l in range(L):
        o = l * PL
        pp[:, o:o + 8] = fm(norm1[l], 8)
        pp[:, o + 8:o + 16] = fm(norm2[l], 8)
        pp[:, o + 16:o + 64] = fm(b_ada[l], 48)
        pp[:, o + 64] = np.tile(f(gqa_q_norm[l]), 2)
        pp[:, o + 65] = np.tile(f(gqa_k_norm[l]), 2)
        pp[:, o + 66] = f(diff_subln[l])
        cw = f(conv_dw[l])
        pp[:, o + 67:o + 191] = cw.T.reshape(4, 128, 31).transpose(1, 0, 2).reshape(128, 124)
        pp[:, o + 191:o + 195] = fm(conv_dw_b[l], 4)
        pp[:, o + 195:o + 199] = fm(conv_ln_g[l], 4)
        pp[:, o + 199:o + 203] = fm(conv_ln_b[l], 4)
        for i, v in enumerate((diff_lq1, diff_lk1, diff_lq2, diff_lk2)):
            pp[:, o + 203 + i * 64:o + 203 + (i + 1) * 64] = np.broadcast_to(f(v[l])[None, :], (128, 64))
    pp[:, 2 * PL:2 * PL + 8] = fm(final_norm, 8)
    mats = np.zeros((128, 384), np.float32)
    for b0 in (0, 64):
        for i in range(16):
            mats[b0 + 16 + i, b0 + i] = -1.0
            mats[b0 + i, b0 + 16 + i] = 1.0
            mats[b0 + 48 + i, b0 + 32 + i] = -1.0
            mats[b0 + 32 + i, b0 + 48 + i] = 1.0
        mats[b0:b0 + 64, 128 + b0:128 + b0 + 64] = 1.0
    mats[:, 256:384] = 1.0
    ws = {k: f(v) for k, v in dict(w_ada=w_ada, w_in=w_in, w_diff_o=w_diff_o, w_gqa_o=w_gqa_o, w_conv_o=w_conv_o, w_o=w_o,
                                   w_mlp1=w_mlp1, w_mlp2=w_mlp2).items()}
    in_maps = []
    for r in range(8):
        sq, q = r // 4, r % 4
        xt = np.concatenate([x_prompt[2 * r], x_prompt[2 * r + 1], x_sample[sq, q * 512:(q + 1) * 512]], 0)
        cT = np.stack([c_ctx.reshape(8, 128).T, c[sq].reshape(8, 128).T], -1).reshape(128, 16)
        cos, sin = _rope_tables(np.arange(q * 512, (q + 1) * 512))
        cs = np.concatenate([np.tile(cos.T, (2, 1)), np.tile(sin.T, (2, 1))], 1)
        masks = np.zeros((128, 18), np.float32)
        masks[:, r] = 1.0
        masks[:, 8 + sq] = 1.0
        if q > 0:
            masks[:, 10 + q - 1] = 1.0
        if q < 3:
            masks[:, 14 + q + 1] = 1.0
        ckd = cdk[sq].reshape(L, 256, 512).transpose(0, 2, 1)
        ckg = cgk[sq].transpose(0, 2, 3, 1)
        ckg = np.concatenate([ckg, ckg], 2)
        cvd = cdv[sq].reshape(L, 256, 512)
        cvg = np.concatenate([cgv[sq], cgv[sq]], -1).reshape(L, 256, 256)
        xs_l, cs_l = [], []
        masks[:] = 0.0
        for i in range(4):
            g_ = (q + i) % 4
            xs_l.append(x_sample[sq, g_ * 512:(g_ + 1) * 512].T)
            co, si = _rope_tables(np.arange(g_ * 512, (g_ + 1) * 512))
            cs_l.append(np.concatenate([np.tile(co.T, (2, 1)), np.tile(si.T, (2, 1))], 1))
            masks[:, i] = 1.0 if g_ > 0 else 0.0
            masks[:, 4 + i] = 1.0 if g_ < 3 else 0.0
        m = dict(xT=f(xt.T), cT=f(cT), pp=pp, cs=f(cs), mats=mats, masks=masks, ckd=f(ckd), ckg=f(ckg), cvd=f(cvd), cvg=f(cvg),
                 xsT=f(np.stack(xs_l)), cs4=f(np.stack(cs_l)))
        m.update(ws)
        in_maps.append(m)
    nc = build()
    res = run_bass_kernel_spmd(nc, in_maps, core_ids=list(range(8)))
    y_prompt = np.zeros((16, 256, D), np.float32)
    y_sample = np.zeros((2, 2048, D), np.float32)
    ndk = np.zeros((16, L, 256, 4, 2, 64), np.float32)
    ndv = np.zeros((16, L, 256, 4, 128), np.float32)
    ngk = np.zeros((16, L, 256, 2, 64), np.float32)
    ngv = np.zeros((16, L, 256, 2, 64), np.float32)
    for r in range(8):
        o = res.results[r]
        sq, q = r // 4, r % 4
        y = np.asarray(o["yT"]).T
        y_prompt[2 * r] = y[0:256]
        y_prompt[2 * r + 1] = y[256:512]
        y_sample[sq, q * 512:(q + 1) * 512] = y[512:1024]
        for i in range(2):
            ndk[2 * r + i] = np.asarray(o["okd"])[:, :, i * 256:(i + 1) * 256].transpose(0, 2, 1).reshape(L, 256, 4, 2, 64)
            ngk[2 * r + i] = np.asarray(o["okg"])[:, :, i * 256:(i + 1) * 256].transpose(0, 2, 1).reshape(L, 256, 2, 64)
            ndv[2 * r + i] = np.asarray(o["ovd"])[:, i * 256:(i + 1) * 256, :].reshape(L, 256, 4, 128)
            ngv[2 * r + i] = np.asarray(o["ovg"])[:, i * 256:(i + 1) * 256, :].reshape(L, 256, 2, 64)
    return (y_prompt, y_sample, ndk, ndv, ngk, ngv)
```
